# Optimizing a Trainium2 kernel written in Bass

```python
import jax, jax.numpy as jnp
from jax import lax
import numpy as np

D_MODEL = 1024
BATCH = 32
SEQ = 2048
DEPTH = 4

GRID_W = 64
CTX_LEN = 256
HEAD_DIM = 64
NA_HEADS = 6
NA_WIN_ROWS = 8
NA_WIN_COLS = 16
GQA_Q_HEADS = 6
GQA_KV_HEADS = 2
GQA_GROUP = GQA_Q_HEADS // GQA_KV_HEADS
Q_BLOCK = 128
ROPE_THETA = 10000.0
POOL_WINDOWS = (2, 4, 8, 16)
POOL_GROUP_DIM = 64
POOL_GROUPS = len(POOL_WINDOWS)
POOL_WIDTH = POOL_GROUPS * POOL_GROUP_DIM

NA_WIDTH = NA_HEADS * HEAD_DIM
GQA_WIDTH = GQA_Q_HEADS * HEAD_DIM
GQA_KV_WIDTH = GQA_KV_HEADS * HEAD_DIM
N_BRANCH = 3
D_FF = -(-8 * D_MODEL // (3 * 256)) * 256
N_MOD = 6
EPS = 1e-6
NEG_INF = -1e30
IN_SIZES = (NA_WIDTH, NA_WIDTH, NA_WIDTH, GQA_WIDTH, GQA_KV_WIDTH, GQA_KV_WIDTH, POOL_WIDTH, N_BRANCH * D_MODEL)
IN_WIDTH = sum(IN_SIZES)

kernel_name = "hybrid_natten_gqa_pool_diffusion_trunk"


def rms_norm(x, g):
    xf = x.astype(jnp.float32)
    y = xf * lax.rsqrt(jnp.mean(xf * xf, axis=-1, keepdims=True) + EPS)
    return (y * g.astype(jnp.float32)).astype(x.dtype)


def modulate(x, shift, scale):
    return x * (1 + scale) + shift


def axial_rope(x, row, col):
    quarter = HEAD_DIM // 4
    half = HEAD_DIM // 2
    freqs = ROPE_THETA ** (-jnp.arange(quarter, dtype=jnp.float32) / quarter)

    def rot(xa, pos):
        ang = pos.astype(jnp.float32)[:, None] * freqs[None, :]
        cos = jnp.cos(ang)[None, :, None, :]
        sin = jnp.sin(ang)[None, :, None, :]
        x1 = xa[..., :quarter].astype(jnp.float32)
        x2 = xa[..., quarter:].astype(jnp.float32)
        return jnp.concatenate([x1 * cos - x2 * sin, x2 * cos + x1 * sin], axis=-1)

    out = jnp.concatenate([rot(x[..., :half], row), rot(x[..., half:], col)], axis=-1)
    return out.astype(x.dtype)


def dense_attention(q, k, v):
    s = jnp.einsum('bqhgd,bkhd->bhgqk', q, k).astype(jnp.float32) * (HEAD_DIM ** -0.5)
    p = jax.nn.softmax(s, axis=-1).astype(v.dtype)
    return jnp.einsum('bhgqk,bkhd->bqhgd', p, v)


def gqa_block_attention(q, k, v):
    B, S = q.shape[0], q.shape[1]
    nb = S // Q_BLOCK
    qb = q.reshape(B, nb, Q_BLOCK, GQA_KV_HEADS, GQA_GROUP, HEAD_DIM).swapaxes(0, 1)
    out = lax.map(lambda qi: dense_attention(qi, k, v), qb)
    return out.swapaxes(0, 1).reshape(B, S, GQA_WIDTH)


def neighborhood_attention(q, k, v, k_ctx, v_ctx, rpb):
    B, S, H, dh = q.shape
    rows = S // GRID_W
    wr = min(NA_WIN_ROWS, rows)
    qg = q.reshape(B, rows, GRID_W, H, dh)
    kg = k.reshape(B, rows, GRID_W, H, dh)
    vg = v.reshape(B, rows, GRID_W, H, dh)
    cq = jnp.arange(GRID_W)
    col_start = jnp.clip(cq - NA_WIN_COLS // 2, 0, GRID_W - NA_WIN_COLS)
    col_valid = (cq[None, :] >= col_start[:, None]) & (cq[None, :] < col_start[:, None] + NA_WIN_COLS)
    dc_idx = jnp.clip(cq[None, :] - cq[:, None] + NA_WIN_COLS - 1, 0, 2 * NA_WIN_COLS - 2)
    scale = dh ** -0.5

    def row_step(r):
        rs = jnp.clip(r - wr // 2, 0, rows - wr)
        k_rows = lax.dynamic_slice_in_dim(kg, rs, wr, axis=1)
        v_rows = lax.dynamic_slice_in_dim(vg, rs, wr, axis=1)
        q_row = lax.dynamic_index_in_dim(qg, r, axis=1, keepdims=False)
        s_loc = jnp.einsum('bqhd,brkhd->bhqrk', q_row, k_rows).astype(jnp.float32) * scale
        dr = rs + jnp.arange(wr) - r + NA_WIN_ROWS - 1
        bias = rpb[:, dr[None, :, None], dc_idx[:, None, :]].astype(jnp.float32)
        s_loc = jnp.where(col_valid[:, None, :], s_loc + bias[None], NEG_INF)
        s_ctx = jnp.einsum('bqhd,blhd->bhql', q_row, k_ctx).astype(jnp.float32) * scale
        s = jnp.concatenate([s_loc.reshape(B, H, GRID_W, wr * GRID_W), s_ctx], axis=-1)
        p = jax.nn.softmax(s, axis=-1).astype(v.dtype)
        p_loc = p[..., :wr * GRID_W].reshape(B, H, GRID_W, wr, GRID_W)
        p_ctx = p[..., wr * GRID_W:]
        return (jnp.einsum('bhqrk,brkhd->bqhd', p_loc, v_rows)
                + jnp.einsum('bhql,blhd->bqhd', p_ctx, v_ctx))

    out = lax.map(row_step, jnp.arange(rows))
    return jnp.moveaxis(out, 0, 1).reshape(B, S, H * dh)


def multiscale_pool(v, w_pool, scale):
    B, N, _ = v.shape
    vf = v.astype(jnp.float32)
    cs = jnp.concatenate([jnp.zeros((B, 1, POOL_WIDTH), jnp.float32), jnp.cumsum(vf, axis=1)], axis=1)
    t = jnp.arange(N)
    means = []
    for g, win in enumerate(POOL_WINDOWS):
        lo = jnp.clip(t - win // 2, 0, N)
        hi = jnp.clip(t + win // 2, 0, N)
        csg = cs[..., g * POOL_GROUP_DIM:(g + 1) * POOL_GROUP_DIM]
        means.append((csg[:, hi] - csg[:, lo]) / (hi - lo).astype(jnp.float32)[None, :, None])
    pooled = (jnp.concatenate(means, axis=-1) - vf).astype(v.dtype)
    pooled = pooled.reshape(B, N, POOL_GROUPS, POOL_GROUP_DIM)
    y = jnp.einsum('bngc,gcd->bngd', pooled, w_pool).reshape(B, N, POOL_WIDTH)
    return y * scale


def project(u, w_in, qn_a, kn_a, qn_b, kn_b):
    B, N, _ = u.shape
    z = u @ w_in
    pieces = []
    off = 0
    for sz in IN_SIZES:
        pieces.append(z[..., off:off + sz])
        off += sz
    qa, ka, va, qb, kb, vb, pc, gates = pieces
    qa = rms_norm(qa.reshape(B, N, NA_HEADS, HEAD_DIM), qn_a)
    ka = rms_norm(ka.reshape(B, N, NA_HEADS, HEAD_DIM), kn_a)
    va = va.reshape(B, N, NA_HEADS, HEAD_DIM)
    qb = rms_norm(qb.reshape(B, N, GQA_Q_HEADS, HEAD_DIM), qn_b)
    kb = rms_norm(kb.reshape(B, N, GQA_KV_HEADS, HEAD_DIM), kn_b)
    vb = vb.reshape(B, N, GQA_KV_HEADS, HEAD_DIM)
    return qa, ka, va, qb, kb, vb, pc, gates


def merge_branches(ya, yb, yc, gates, w_br_a, w_br_b, w_br_c, w_out):
    g = jax.nn.sigmoid(gates)
    ga = g[..., :D_MODEL]
    gb = g[..., D_MODEL:2 * D_MODEL]
    gc = g[..., 2 * D_MODEL:]
    m = ga * (ya @ w_br_a) + gb * (yb @ w_br_b) + gc * (yc @ w_br_c)
    return m @ w_out


def swiglu(u, w1, w3, w2):
    return (jax.nn.silu(u @ w1) * (u @ w3)) @ w2


def setup_inputs(seed: int = 0) -> dict:
    key = jax.random.key(seed)
    ks = jax.random.split(key, 32)
    f32 = jnp.float32

    def nrm(k, shape, scale):
        return jax.random.normal(k, shape, f32) * scale

    L = DEPTH
    D = D_MODEL
    return {
        "x": nrm(ks[0], (BATCH, SEQ, D), 1.0),
        "c": nrm(ks[1], (BATCH, D), 1.0),
        "ctx": nrm(ks[2], (BATCH, CTX_LEN, D), 1.0),
        "c_ctx": nrm(ks[3], (D,), 1.0),
        "w_mod": nrm(ks[4], (L, D, N_MOD * D), 0.5 * D ** -0.5),
        "b_mod": nrm(ks[5], (L, N_MOD * D), 0.01),
        "norm1_g": 1.0 + nrm(ks[6], (L, D), 0.02),
        "norm2_g": 1.0 + nrm(ks[7], (L, D), 0.02),
        "w_in": nrm(ks[8], (L, D, IN_WIDTH), D ** -0.5),
        "q_norm_a": 1.0 + nrm(ks[9], (L, HEAD_DIM), 0.02),
        "k_norm_a": 1.0 + nrm(ks[10], (L, HEAD_DIM), 0.02),
        "q_norm_b": 1.0 + nrm(ks[11], (L, HEAD_DIM), 0.02),
        "k_norm_b": 1.0 + nrm(ks[12], (L, HEAD_DIM), 0.02),
        "rpb_a": nrm(ks[13], (L, NA_HEADS, 2 * NA_WIN_ROWS - 1, 2 * NA_WIN_COLS - 1), 0.2),
        "w_pool": nrm(ks[14], (L, POOL_GROUPS, POOL_GROUP_DIM, POOL_GROUP_DIM), POOL_GROUP_DIM ** -0.5),
        "pool_scale": 1.0 + nrm(ks[15], (L, POOL_WIDTH), 0.1),
        "w_br_a": nrm(ks[16], (L, NA_WIDTH, D), NA_WIDTH ** -0.5),
        "w_br_b": nrm(ks[17], (L, GQA_WIDTH, D), GQA_WIDTH ** -0.5),
        "w_br_c": nrm(ks[18], (L, POOL_WIDTH, D), POOL_WIDTH ** -0.5),
        "w_out": nrm(ks[19], (L, D, D), D ** -0.5),
        "w_ff1": nrm(ks[20], (L, D, D_FF), D ** -0.5),
        "w_ff3": nrm(ks[21], (L, D, D_FF), D ** -0.5),
        "w_ff2": nrm(ks[22], (L, D_FF, D), D_FF ** -0.5),
    }


def reference(x, c, ctx, c_ctx, w_mod, b_mod, norm1_g, norm2_g, w_in, q_norm_a, k_norm_a,
              q_norm_b, k_norm_b, rpb_a, w_pool, pool_scale, w_br_a, w_br_b, w_br_c, w_out,
              w_ff1, w_ff3, w_ff2):
    B, S, _ = x.shape
    Lc = ctx.shape[1]
    t = jnp.arange(S)
    row = t // GRID_W
    col = t % GRID_W
    h = ctx
    sc = jax.nn.silu(c)
    sh = jax.nn.silu(c_ctx)
    for l in range(DEPTH):
        last = l == DEPTH - 1
        mod_x = (sc @ w_mod[l] + b_mod[l])[:, None, :]
        mod_h = sh @ w_mod[l] + b_mod[l]
        sx1, cx1, gx1, sx2, cx2, gx2 = jnp.split(mod_x, N_MOD, axis=-1)
        sh1, ch1, gh1, sh2, ch2, gh2 = jnp.split(mod_h, N_MOD, axis=-1)

        ux = modulate(rms_norm(x, norm1_g[l]), sx1, cx1)
        uh = modulate(rms_norm(h, norm1_g[l]), sh1, ch1)
        qa, ka, va, qb, kb, vb, pc, gates = project(ux, w_in[l], q_norm_a[l], k_norm_a[l], q_norm_b[l], k_norm_b[l])
        qa_h, ka_h, va_h, qb_h, kb_h, vb_h, pc_h, gates_h = project(uh, w_in[l], q_norm_a[l], k_norm_a[l], q_norm_b[l], k_norm_b[l])

        ya = neighborhood_attention(qa, ka, va, ka_h, va_h, rpb_a[l])
        qb_r = axial_rope(qb, row, col).reshape(B, S, GQA_KV_HEADS, GQA_GROUP, HEAD_DIM)
        kb_r = axial_rope(kb, row, col)
        yb = gqa_block_attention(qb_r, jnp.concatenate([kb_h, kb_r], axis=1),
                                 jnp.concatenate([vb_h, vb], axis=1))
        yc = multiscale_pool(pc, w_pool[l], pool_scale[l])
        x = x + gx1 * merge_branches(ya, yb, yc, gates, w_br_a[l], w_br_b[l], w_br_c[l], w_out[l])

        u2 = modulate(rms_norm(x, norm2_g[l]), sx2, cx2)
        x = x + gx2 * swiglu(u2, w_ff1[l], w_ff3[l], w_ff2[l])

        if not last:
            ya_h = dense_attention(qa_h[:, :, :, None, :], ka_h, va_h).reshape(B, Lc, NA_WIDTH)
            yb_h = dense_attention(qb_h.reshape(B, Lc, GQA_KV_HEADS, GQA_GROUP, HEAD_DIM), kb_h, vb_h).reshape(B, Lc, GQA_WIDTH)
            yc_h = multiscale_pool(pc_h, w_pool[l], pool_scale[l])
            h = h + gh1 * merge_branches(ya_h, yb_h, yc_h, gates_h, w_br_a[l], w_br_b[l], w_br_c[l], w_out[l])
            u2h = modulate(rms_norm(h, norm2_g[l]), sh2, ch2)
            h = h + gh2 * swiglu(u2h, w_ff1[l], w_ff3[l], w_ff2[l])
    return x
```

```python
import numpy as np
from contextlib import ExitStack
import concourse.bass as bass
import concourse.mybir as mybir
from concourse.bass_utils import run_bass_kernel_spmd

F32 = mybir.dt.float32
BF16 = mybir.dt.bfloat16
AF = mybir.ActivationFunctionType
ALU = mybir.AluOpType
AX = mybir.AxisListType

NCORES = 8
BL = 4
S = 2048
LC = 256
NT = S + LC
D = 1024
DFF = 2816
DEPTH = 4
EPS = 1e-6
ENGS = ("pe", "act", "dve", "pool", "sp")


class Reg:
    __slots__ = ("name", "ws", "rs", "war")

    def __init__(self, name):
        self.name = name
        self.ws = []
        self.rs = []
        self.war = []


class Op:
    __slots__ = ("eng", "emit", "deps", "marked", "ev", "dma")

    def __init__(self, eng, emit, dma=None):
        self.eng = eng
        self.emit = emit
        self.deps = []
        self.marked = False
        self.ev = None
        self.dma = dma


class V:
    __slots__ = ("ap", "reg")

    def __init__(self, ap, reg):
        self.ap = ap
        self.reg = reg

    def __getitem__(self, k):
        return V(self.ap[k], self.reg)

    def rr(self, pat, **kw):
        return V(self.ap.rearrange(pat, **kw), self.reg)

    def bc(self, axis, shape):
        return V(self.ap.unsqueeze(axis).to_broadcast(shape), self.reg)


class Prog:
    def __init__(self, nc):
        self.nc = nc
        self.ops = {e: [] for e in ENGS}

    def add(self, eng, emit, reads=(), writes=(), dma=None, concurrent=False):
        op = Op(eng, emit, dma)
        deps = op.deps
        for r in reads:
            deps.extend(r.ws)
            r.rs.append(op)
        for w in writes:
            if concurrent:
                if w.rs:
                    w.war = w.rs
                    w.rs = []
                    w.ws = []
                deps.extend(w.war)
                w.ws.append(op)
            else:
                deps.extend(w.ws)
                deps.extend(w.rs)
                deps.extend(w.war)
                w.ws = [op]
                w.rs = []
                w.war = []
        op.deps = [d for d in deps if d is not op]
        if eng == "pe":
            op.deps = [d for d in op.deps if not (d.eng == "pe" and d.dma is None)]
        self.ops[eng].append(op)
        return op

    def dma(self, out, in_, key=None, concurrent=False, q=None):
        k = key or out.reg.name
        if q is None:
            q = "sp" if out.reg.name.startswith("sb:") else "pool"
        return self.add(q, lambda e: e.dma_start(out=out.ap, in_=in_.ap), [in_.reg], [out.reg], dma=k, concurrent=concurrent)

    def mm(self, out, lhsT, rhs, start, stop):
        return self.add("pe", lambda e: e.matmul(out.ap, lhsT.ap, rhs.ap, start=start, stop=stop), [lhsT.reg, rhs.reg], [out.reg])

    def tr(self, out, in_, ident):
        return self.add("pe", lambda e: e.transpose(out=out.ap, in_=in_.ap, identity=ident.ap), [in_.reg, ident.reg], [out.reg])

    def act(self, out, in_, func, scale=1.0):
        return self.add("act", lambda e: e.activation(out=out.ap, in_=in_.ap, func=func, scale=scale), [in_.reg], [out.reg])

    def bcv(self, v, axis, shape):
        return V(v.ap.unsqueeze(axis).to_broadcast(shape), v.reg)

    def tt(self, out, a, b, op, eng="dve"):
        return self.add(eng, lambda e: e.tensor_tensor(out=out.ap, in0=a.ap, in1=b.ap, op=op), [a.reg, b.reg], [out.reg])

    def ts(self, out, a, s1, s2, op0=ALU.mult, op1=ALU.add):
        return self.add("dve", lambda e: e.tensor_scalar(out=out.ap, in0=a.ap, scalar1=s1, scalar2=s2, op0=op0, op1=op1), [a.reg], [out.reg])

    def stt(self, out, a, sc, b, op0, op1):
        return self.add("dve", lambda e: e.scalar_tensor_tensor(out=out.ap, in0=a.ap, scalar=sc.ap, in1=b.ap, op0=op0, op1=op1),
                        [a.reg, sc.reg, b.reg], [out.reg])

    def cp(self, out, in_, eng="dve"):
        if eng == "act":
            return self.act(out, in_, AF.Copy)
        return self.add("dve", lambda e: e.tensor_copy(out=out.ap, in_=in_.ap), [in_.reg], [out.reg])

    def red(self, out, in_):
        return self.add("dve", lambda e: e.tensor_reduce(out=out.ap, in_=in_.ap, axis=AX.X, op=ALU.add), [in_.reg], [out.reg])

    def rcp(self, out, in_):
        return self.add("dve", lambda e: e.reciprocal(out=out.ap, in_=in_.ap), [in_.reg], [out.reg])

    def memset(self, out, val):
        return self.add("dve", lambda e: e.memset(out.ap, val), [], [out.reg])

    def finalize(self, stack, final_waits=()):
        nc = self.nc
        esem = {e: stack.enter_context(nc.semaphore("s_" + e)) for e in ENGS}
        for e in ENGS:
            for op in self.ops[e]:
                for d in op.deps:
                    d.marked = True
        for op in final_waits:
            op.marked = True
        dsem = {}
        for e in ENGS:
            c = 0
            for op in self.ops[e]:
                if op.dma is not None:
                    if op.dma not in dsem:
                        dsem[op.dma] = [stack.enter_context(nc.semaphore("d_%d" % len(dsem))), 0]
                    s = dsem[op.dma]
                    s[1] += 16
                    op.ev = (s[0], s[1], op.dma)
                elif op.marked:
                    c += 1
                    op.ev = (esem[e], c, e)
        block = stack.enter_context(nc.Block())
        ops = self.ops

        def run(engname, eng, extra=None):
            seen = {}
            for op in ops[engname]:
                need = {}
                for d in op.deps:
                    s, v, k = d.ev
                    if seen.get(k, 0) >= v:
                        continue
                    if k not in need or need[k][1] < v:
                        need[k] = (s, v)
                for k, (s, v) in need.items():
                    eng.wait_ge(s, v)
                    seen[k] = v
                ins = op.emit(eng)
                if op.dma is not None:
                    ins.then_inc(op.ev[0], 16)
                elif op.marked:
                    ins.then_inc(op.ev[0], 1)
            if extra:
                for op in extra:
                    s, v, k = op.ev
                    if seen.get(k, 0) < v:
                        eng.wait_ge(s, v)
                        seen[k] = v

        @block.tensor
        def _(eng):
            run("pe", eng)

        @block.scalar
        def _(eng):
            run("act", eng)

        @block.vector
        def _(eng):
            run("dve", eng)

        @block.gpsimd
        def _(eng):
            run("pool", eng)

        @block.sync
        def _(eng):
            run("sp", eng, extra=final_waits)


class Rot:
    def __init__(self, items):
        self.items = items
        self.i = 0

    def next(self):
        v = self.items[self.i % len(self.items)]
        self.i += 1
        return v


GRID_W = 64
QKW = 1280
INW = 5120
POOL_WINDOWS = (2, 4, 8, 16)


def build_program(depth=DEPTH, bl=BL):
    nc = bass.Bass("TRN2", target_bir_lowering=False)

    def din(name, shape):
        return V(nc.dram_tensor(name, list(shape), F32, kind="ExternalInput").ap(), Reg("in:" + name))

    def dscr(name, shape, dt):
        return V(nc.dram_tensor(name, list(shape), dt).ap(), Reg("dr:" + name))

    xin = din("xin", [bl, NT, D])
    cT = din("cT", [128, 40])
    w_mod = din("w_mod", [depth, D, 6 * D])
    b_mod = din("b_mod", [depth, 6 * D])
    ng12 = din("ng12", [depth, 2 * D])
    w_in = din("w_in", [depth, D, INW])
    qkg = din("qkg", [depth, QKW])
    ropec = din("ropec", [128, 16 * 32])
    ropes = din("ropes", [128, 16 * 32])
    erpb = din("erpb", [depth, 128, 6 * 14 * 64])
    cmask = din("cmask", [128, 64])
    bandm = din("bandm", [128, 20 * 128])
    wpool = din("wpool", [depth, 64, 256])
    pscale = din("pscale", [depth, 256])
    w_br = din("w_br", [depth, D, D])
    w_out = din("w_out", [depth, D, D])
    w_ff1 = din("w_ff1", [depth, D, DFF])
    w_ff3 = din("w_ff3", [depth, D, DFF])
    w_ff2 = din("w_ff2", [depth, DFF, D])
    ident_d = din("ident", [128, 128])
    out = V(nc.dram_tensor("out", [bl, S, D], F32, kind="ExternalOutput").ap(), Reg("dr:out"))
    xres = [dscr("xres%d" % b, [NT, D], F32) for b in range(bl)]
    qkT = [dscr("qkT%d" % b, [QKW, NT], BF16) for b in range(bl)]
    vsc = [dscr("vsc%d" % b, [NT, 520], BF16) for b in range(bl)]
    pcs = [dscr("pcs%d" % b, [NT, 256], BF16) for b in range(bl)]
    gts = [dscr("gts%d" % b, [NT, 3 * D], BF16) for b in range(bl)]
    modd = [dscr("modd%d" % l, [5, 6 * D], F32) for l in range(depth)]

    with ExitStack() as st:
        def sbt(name, shape, dt):
            return st.enter_context(nc.sbuf_tensor("s_" + name, list(shape), dt))

        def sb(name, shape, dt):
            return V(sbt(name, shape, dt)[:], Reg("sb:" + name))

        def ps(name, shape, dt):
            return V(st.enter_context(nc.psum_tensor("p_" + name, list(shape), dt))[:], Reg("ps:" + name))

        P = Prog(nc)
        ARN = 67584
        AR = sbt("AR", [128, ARN], BF16)
        WKF = sbt("WKF", [128, 4608], F32)
        WKB = sbt("WKB", [128, 9216], BF16)
        live = []

        def fence():
            ops = []
            for r in live:
                ops.extend(r.ws)
                ops.extend(r.rs)
                ops.extend(r.war)
            del live[:]
            return ops

        cur_fence = [[]]

        def carve(base, name, p0, p1, o0, n):
            r = Reg("sb:" + name)
            r.war = list(cur_fence[0])
            live.append(r)
            return V(base[p0:p1, o0:o0 + n], r)

        stg = Rot([sb("stg%d" % i, [128, 1024], F32) for i in range(2)])
        modt = sb("modt", [128, 2 * D], F32)
        xt = sb("xt", [128, D], F32)
        t1 = sb("t1", [128, D], F32)
        u = sb("u", [128, D], BF16)
        uT = sb("uT", [128, D], BF16)
        e1 = Rot([sb("e1_%d" % i, [128, 512], F32) for i in range(3)])
        sm = sb("sm", [128, 32], F32)
        ident = sb("ident", [128, 128], BF16)
        csf = sb("csf", [128, 40], F32)
        cse = sb("cse", [128, 40], F32)
        scT = sb("scT", [128, 8 * 32], BF16)
        psA = [ps("psA%d" % i, [128, 512], F32) for i in range(2)]
        psS = [ps("psS%d" % i, [128, 512], F32) for i in range(2)]
        pY = ps("pY", [128, 512], F32)
        pN = ps("pN", [128, 512], F32)
        psT = [ps("psT%d" % i, [128, 1024], BF16) for i in range(2)]
        rA4 = Rot(psA + psS)
        rA2 = Rot(psA)
        rS = Rot(psS)
        rT = Rot(psT)
        finals = []
        rX = Rot([xt, t1])

        s_ = stg.next()
        P.dma(s_[:, 0:128], ident_d)
        P.cp(ident, s_[:, 0:128])
        P.dma(csf, cT)
        P.act(cse, csf, AF.Sigmoid)
        P.memset(scT, 0.0)
        P.tt(scT.rr("p (k j) -> p k j", j=32)[:, :, 0:5], csf.rr("p (k j) -> p k j", j=5), cse.rr("p (k j) -> p k j", j=5), ALU.mult)

        cast_i = [0]

        def load_cast(dst, src, n):
            s_ = stg.next()
            P.dma(s_[0:dst.ap.shape[0], 0:n], src)
            P.cp(dst, s_[0:dst.ap.shape[0], 0:n], eng=("act" if cast_i[0] % 2 else "dve"))
            cast_i[0] += 1

        def load_mat(dst, src2d, nk, ncol):
            for k in range(nk):
                c0 = 0
                while c0 < ncol:
                    n = min(1024, ncol - c0)
                    load_cast(dst[:, k * ncol + c0:k * ncol + c0 + n], src2d[128 * k:128 * k + 128, c0:c0 + n], n)
                    c0 += n

        def norm_mod(x_v, rUT):
            for hf in range(2):
                e = e1.next()
                P.tt(e, x_v[:, 512 * hf:512 * hf + 512], x_v[:, 512 * hf:512 * hf + 512], ALU.mult)
                P.red(sm[:, hf:hf + 1], e.rr("p (o f) -> p o f", o=1))
            P.tt(sm[:, 2:3], sm[:, 0:1], sm[:, 1:2], ALU.add)
            P.ts(sm[:, 3:4], sm[:, 2:3], 1.0 / D, EPS)
            P.act(sm[:, 4:5], sm[:, 3:4], AF.Sqrt)
            P.rcp(sm[:, 5:6], sm[:, 4:5])
            for hf in range(2):
                e = e1.next()
                P.stt(e, x_v[:, 512 * hf:512 * hf + 512], sm[:, 5:6], modt[:, D + 512 * hf:D + 512 * hf + 512], ALU.mult, ALU.mult)
                P.tt(u[:, 512 * hf:512 * hf + 512], e, modt[:, 512 * hf:512 * hf + 512], ALU.add)
            pt = rT.next()
            for k in range(8):
                P.tr(pt[:, 128 * k:128 * k + 128], u[:, 128 * k:128 * k + 128], ident)
            uT_ = rUT.next()
            P.cp(uT_, pt, eng="act")
            return uT_

        def load_mod(l, slot, c0, n, dst):
            P.dma(dst, V(modd[l].ap[slot:slot + 1, c0:c0 + n].partition_broadcast(128), modd[l].reg))

        for l in range(depth):
            last = l == depth - 1
            xsrc = (lambda b: xin[b]) if l == 0 else (lambda b: xres[b])

            cur_fence[0] = fence()
            msec = carve(WKF, "msec", 0, 5, 0, D)
            bsec = carve(WKF, "bsec", 0, 5, D, D)
            g12 = carve(WKF, "g12", 0, 5, 2 * D, 2 * D)
            wm = Rot([carve(WKB, "wm%d" % i, 0, 128, 512 * i, 512) for i in range(2)])
            P.dma(g12, V(ng12.ap[l:l + 1, :].partition_broadcast(5), ng12.reg))
            for sec in range(6):
                c0 = sec * D
                P.dma(bsec, V(b_mod.ap[l:l + 1, c0:c0 + D].partition_broadcast(5), b_mod.reg))
                for g in range(2):
                    pm = rA2.next()
                    for k in range(8):
                        w = wm.next()
                        s_ = stg.next()
                        P.dma(s_[:, 0:512], w_mod[l, 128 * k:128 * k + 128, c0 + 512 * g:c0 + 512 * g + 512])
                        P.cp(w, s_[:, 0:512], eng=("act" if k % 2 else "dve"))
                        P.mm(pm[0:32, :], scT[:, 32 * k:32 * k + 32], w, start=(k == 0), stop=(k == 7))
                    P.tt(msec[:, 512 * g:512 * g + 512], pm[0:5, :], bsec[:, 512 * g:512 * g + 512], ALU.add)
                if sec in (1, 4):
                    gg_ = g12[:, 0:D] if sec == 1 else g12[:, D:2 * D]
                    P.ts(msec, msec, 1.0, 1.0)
                    P.tt(msec, msec, gg_, ALU.mult)
                P.dma(modd[l][:, sec * D:(sec + 1) * D], msec, concurrent=True)

            cur_fence[0] = fence()
            Win = carve(AR, "Win", 0, 128, 0, 8 * INW)
            zf = carve(WKF, "zf", 0, 128, 0, 512)
            sq = carve(WKF, "sq", 0, 128, 512, 512)
            qnf = carve(WKF, "qnf", 0, 128, 1024, QKW)
            QKG = carve(WKF, "QKG", 0, 128, 2304, QKW)
            RC = carve(WKF, "RC", 0, 128, 3584, 512)
            RS = carve(WKF, "RS", 0, 128, 4096, 512)
            qnb = carve(WKB, "qnb", 0, 128, 0, QKW)
            qkTs = carve(WKB, "qkTs", 0, 128, 1280, QKW)
            vt = carve(WKB, "vt", 0, 128, 2560, 520)
            pct = carve(WKB, "pct", 0, 128, 3080, 256)
            gt = carve(WKB, "gt", 0, 128, 3336, 3 * D)
            rUT = Rot([uT, carve(WKB, "uT2", 0, 128, 6600, D)])
            load_mat(Win, w_in[l], 8, INW)
            P.dma(QKG, V(qkg.ap[l:l + 1, :].partition_broadcast(128), qkg.reg))
            P.dma(RC, ropec)
            P.dma(RS, ropes)
            P.memset(vt, 1.0)
            for b in range(bl):
                for part in range(2):
                    slot = b if part == 0 else 4
                    load_mod(l, slot, 0, 2 * D, modt)
                    for i in (range(16) if part == 0 else range(16, 18)):
                        xt_ = rX.next()
                        P.dma(xt_, xsrc(b)[128 * i:128 * i + 128, :])
                        uT_ = norm_mod(xt_, rUT)
                        for g in range(10):
                            p = rA4.next()
                            for k in range(8):
                                P.mm(p, uT_[:, 128 * k:128 * k + 128], Win[:, k * INW + 512 * g:k * INW + 512 * g + 512], start=(k == 0), stop=(k == 7))
                            if g < 3:
                                n = 512 if g < 2 else 256
                                nh = n // 64
                                P.cp(zf[:, 0:n], p[:, 0:n], eng="act")
                                if g == 2:
                                    P.cp(pct, p[:, 256:512], eng="act")
                                    P.dma(pcs[b][128 * i:128 * i + 128, :], pct, concurrent=True)
                                P.tt(sq[:, 0:n], zf[:, 0:n], zf[:, 0:n], ALU.mult)
                                P.red(sm[:, 8:8 + nh], sq[:, 0:n].rr("p (h d) -> p h d", d=64))
                                P.ts(sm[:, 8:8 + nh], sm[:, 8:8 + nh], 1.0 / 64, EPS)
                                P.act(sm[:, 16:16 + nh], sm[:, 8:8 + nh], AF.Sqrt)
                                P.rcp(sm[:, 24:24 + nh], sm[:, 16:16 + nh])
                                P.tt(sq[:, 0:n].rr("p (h d) -> p h d", d=64), zf[:, 0:n].rr("p (h d) -> p h d", d=64),
                                     P.bcv(sm[:, 24:24 + nh], 2, [128, nh, 64]), ALU.mult)
                                P.tt(qnf[:, 512 * g:512 * g + n], sq[:, 0:n], QKG[:, 512 * g:512 * g + n], ALU.mult)
                            elif g == 3:
                                P.cp(vt.rr("p (h d) -> p h d", d=65)[:, :, 0:64], p.rr("p (h d) -> p h d", d=64), eng="act")
                                P.dma(vsc[b][128 * i:128 * i + 128, :], vt, concurrent=True)
                            else:
                                P.act(gt[:, 512 * (g - 4):512 * (g - 4) + 512], p, AF.Sigmoid)
                        P.dma(gts[b][128 * i:128 * i + 128, :], gt, concurrent=True)
                        P.cp(qnb, qnf)
                        if part == 0:
                            cosv = RC[:, 32 * i:32 * i + 32].rr("p (a f) -> p a f", a=2)
                            sinv = RS[:, 32 * i:32 * i + 32].rr("p (a f) -> p a f", a=2)
                            for (c0, H) in ((384, 6), (1152, 2)):
                                shp = [128, H, 2, 16]
                                X = qnf[:, c0:c0 + 64 * H].rr("p (h a t f) -> p h a t f", a=2, t=2, f=16)
                                O = qnb[:, c0:c0 + 64 * H].rr("p (h a t f) -> p h a t f", a=2, t=2, f=16)
                                x1, x2 = X[:, :, :, 0, :], X[:, :, :, 1, :]
                                cb = V(cosv.ap.unsqueeze(1).to_broadcast(shp), cosv.reg)
                                sbv = V(sinv.ap.unsqueeze(1).to_broadcast(shp), sinv.reg)
                                ta = zf[:, 0:32 * H].rr("p (h a f) -> p h a f", a=2, f=16)
                                tb = sq[:, 0:32 * H].rr("p (h a f) -> p h a f", a=2, f=16)
                                P.tt(ta, x1, cb, ALU.mult)
                                P.tt(tb, x2, sbv, ALU.mult)
                                P.tt(O[:, :, :, 0, :], ta, tb, ALU.subtract)
                                P.tt(ta, x2, cb, ALU.mult)
                                P.tt(tb, x1, sbv, ALU.mult)
                                P.tt(O[:, :, :, 1, :], ta, tb, ALU.add)
                        pt = rT.next()
                        for c in range(8):
                            P.tr(pt[:, 128 * c:128 * c + 128], qnb[:, 128 * c:128 * c + 128], ident)
                        P.cp(qkTs[:, 0:1024], pt, eng="act")
                        pt = rT.next()
                        for c in range(2):
                            P.tr(pt[:, 128 * c:128 * c + 128], qnb[:, 1024 + 128 * c:1024 + 128 * c + 128], ident)
                        P.cp(qkTs[:, 1024:1280], pt[:, 0:256], eng="act")
                        P.dma(V(qkT[b].ap[:, 128 * i:128 * i + 128].rearrange("(c p) t -> p c t", p=128), qkT[b].reg),
                              qkTs.rr("p (c t) -> p c t", t=128), concurrent=True)

            cur_fence[0] = fence()
            o = 0
            Wbr = carve(AR, "Wbr", 0, 128, o, 8 * D); o += 8 * D
            Wout = carve(AR, "Wout", 0, 128, o, 8 * D); o += 8 * D
            kTs = carve(AR, "kTs", 0, 64, o, 8 * NT); o += 8 * NT
            vS = carve(AR, "vS", 0, 128, o, 18 * 520); o += 18 * 520
            pcS = carve(AR, "pcS", 0, 128, o, 18 * 256); o += 18 * 256
            Et = carve(AR, "Et", 0, 128, o, 6 * 14 * 64); o += 6 * 14 * 64
            MT = carve(AR, "MT", 0, 128, o, 20 * 128); o += 20 * 128
            wpl = carve(AR, "wpl", 0, 64, o, 256); o += 256
            pTb = [carve(AR, "pT%d" % i, 0, 128, o + 2304 * i, 2304) for i in range(2)]; o += 4608
            vwin = [carve(AR, "vwin%d" % i, 0, 128, o + 1560 * i, 1560) for i in range(2)]; o += 3120
            qTs = carve(AR, "qTs", 0, 64, o, 12 * 128); o += 12 * 128
            assert o <= ARN
            psc = carve(WKF, "psc", 0, 128, 0, 256)
            cmk = carve(WKF, "cmk", 0, 128, 256, 64)
            GXt = carve(WKF, "GXt", 0, 128, 512, D)
            o = 0
            gtB = carve(WKB, "gtB", 0, 128, o, 3 * D); o += 3 * D
            Yt = carve(WKB, "Yt", 0, 128, o, D); o += D
            YTs = carve(WKB, "YTs", 0, 128, o, D); o += D
            mb = carve(WKB, "mb", 0, 128, o, D); o += D
            mTs = carve(WKB, "mTs", 0, 128, o, D); o += D
            pnb = [carve(WKB, "pn%d" % i, 0, 128, o + 384 * i, 384) for i in range(2)]; o += 768
            yhb = [carve(WKB, "yh%d" % i, 0, 64, o + 384 * i, 384) for i in range(2)]; o += 768
            plT = carve(WKB, "plT", 0, 64, o, 512); o += 512
            assert o <= 9216
            rpn, ryh, rpT, rvw = Rot(pnb), Rot(yhb), Rot(pTb), Rot(vwin)
            load_mat(Wbr, w_br[l], 8, D)
            load_mat(Wout, w_out[l], 8, D)
            load_cast(wpl, wpool[l], 256)
            for j in range(3):
                n = min(1024, 2560 - 1024 * j)
                load_cast(MT[:, 1024 * j:1024 * j + n], bandm[:, 1024 * j:1024 * j + n], n)
            P.dma(psc, V(pscale.ap[l:l + 1, :].partition_broadcast(128), pscale.reg))
            P.dma(cmk, cmask)
            for j in range(6):
                s_ = stg.next()
                P.dma(s_[:, 0:896], erpb[l, :, 896 * j:896 * j + 896])
                e_ = e1.next()
                for hh in range(2):
                    P.act(e_[:, 0:448], s_[:, 448 * hh:448 * hh + 448], AF.Exp)
                    P.tt(Et[:, 896 * j + 448 * hh:896 * j + 448 * hh + 448].rr("p (r q) -> p r q", q=64),
                         e_[:, 0:448].rr("p (r q) -> p r q", q=64), P.bcv(cmk, 1, [128, 7, 64]), ALU.mult)
            E4 = Et.rr("p (h r q) -> p h r q", h=6, r=14)

            def dense(qh0, kh0, voff, group, chunks, ycol0):
                nck = len(chunks)

                def scores(h):
                    kh = kh0 + h // group
                    pT = rpT.next()
                    for b0 in range(0, nck, 4):
                        blk = chunks[b0:b0 + 4]
                        p = rS.next()
                        for j, kc in enumerate(blk):
                            P.mm(p[:, 128 * j:128 * j + 128], kTs[:, kh * NT + 128 * kc:kh * NT + 128 * kc + 128], qTs[:, 128 * (qh0 + h):128 * (qh0 + h) + 128],
                                 start=True, stop=True)
                        P.act(pT[:, 128 * b0:128 * b0 + 128 * len(blk)], p[:, 0:128 * len(blk)], AF.Exp, scale=0.125)
                    return pT

                def pvm(h, pT):
                    vcol = voff + 65 * (h // group)
                    for idx, kc in enumerate(chunks):
                        P.mm(pY[:, 65 * h:65 * h + 65], pT[:, 128 * idx:128 * idx + 128], vS[:, 520 * kc + vcol:520 * kc + vcol + 65],
                             start=(idx == 0), stop=(idx == nck - 1))

                prev = None
                for h in range(6):
                    pT = scores(h)
                    if prev is not None:
                        pvm(*prev)
                    prev = (h, pT)
                pvm(*prev)
                pv = pY[:, 0:390].rr("p (h d) -> p h d", d=65)
                P.rcp(sm[:, 8:14], pv[:, :, 64])
                P.tt(Yt[:, ycol0:ycol0 + 384].rr("p (h d) -> p h d", d=64), pv[:, :, 0:64], P.bcv(sm[:, 8:14], 2, [128, 6, 64]), ALU.mult)

            def phA(b, part, i):
                P.dma(qTs.rr("d (j t) -> d j t", j=12), V(qkT[b].ap[0:768, 128 * i:128 * i + 128].rearrange("(j d) t -> d j t", d=64), qkT[b].reg))
                xt_ = rX.next()
                P.dma(xt_, xsrc(b)[128 * i:128 * i + 128, :])
                if part == 0:
                    dense(6, 6, 390, 3, list(range(18)), 384)
                else:
                    dense(6, 6, 390, 3, [16, 17], 384)
                    dense(0, 0, 0, 1, [16, 17], 0)
                return xt_

            def phB(b, part, i):
                ptY = psT[0]
                if part == 0:
                    for half in range(2):
                        r = 2 * i + half
                        rs_ = min(max(r - 4, 0), 24)
                        off = rs_ - r + 7
                        tok0 = 64 * rs_
                        vw = rvw.next()
                        P.dma(vw.rr("p (c f) -> p c f", f=390), V(vsc[b].ap[tok0:tok0 + 512, 0:390].rearrange("(c p) f -> p c f", p=128), vsc[b].reg))
                        def na_scores(h):
                            p = rS.next()
                            for c in range(6):
                                t0 = tok0 + 128 * c if c < 4 else S + 128 * (c - 4)
                                P.mm(p[:, 64 * c:64 * c + 64], kTs[:, h * NT + t0:h * NT + t0 + 128], qTs[:, 128 * h + 64 * half:128 * h + 64 * half + 64],
                                     start=True, stop=True)
                            pn = rpn.next()
                            P.act(pn, p[:, 0:384], AF.Exp, scale=0.125)
                            P.tt(pn[:, 0:256].rr("p (c q) -> p c q", q=64), pn[:, 0:256].rr("p (c q) -> p c q", q=64),
                                 V(E4.ap[:, h, off:off + 7:2, :], E4.reg), ALU.mult)
                            return pn

                        def na_pv(h, pn):
                            for c in range(6):
                                rhs = vw[:, 390 * c + 65 * h:390 * c + 65 * h + 65] if c < 4 else vS[:, 520 * (16 + c - 4) + 65 * h:520 * (16 + c - 4) + 65 * h + 65]
                                P.mm(pN[0:64, 65 * h:65 * h + 65], pn[:, 64 * c:64 * c + 64], rhs, start=(c == 0), stop=(c == 5))

                        prev = None
                        for h in range(6):
                            pn = na_scores(h)
                            if prev is not None:
                                na_pv(*prev)
                            prev = (h, pn)
                        na_pv(*prev)
                        pv = pN[0:64, 0:390].rr("p (h d) -> p h d", d=65)
                        P.rcp(sm[0:64, 16:22], pv[:, :, 64])
                        yh = ryh.next()
                        P.tt(yh.rr("p (h d) -> p h d", d=64), pv[:, :, 0:64], P.bcv(sm[0:64, 16:22], 2, [64, 6, 64]), ALU.mult)
                        for cc in range(3):
                            P.tr(ptY[:, 128 * cc + 64 * half:128 * cc + 64 * half + 64], yh[:, 128 * cc:128 * cc + 128], ident[0:64, 0:64])
                if part == 0:
                    ty = 0 if i == 0 else (2 if i == 15 else 1)
                    prv = i - 1 if i > 0 else None
                    nxt = i + 1 if i < 15 else None
                else:
                    ty = 0 if i == 16 else 2
                    prv = 16 if i == 17 else None
                    nxt = 17 if i == 16 else None
                pp = rA2.next()
                for g in range(4):
                    terms = [(i, 5 * g + ty)]
                    if prv is not None:
                        terms.append((prv, 5 * g + 3))
                    if nxt is not None:
                        terms.append((nxt, 5 * g + 4))
                    for ti, (tile_, mi) in enumerate(terms):
                        P.mm(pp[0:64, 128 * g:128 * g + 128], pcS[:, 256 * tile_ + 64 * g:256 * tile_ + 64 * g + 64], MT[:, 128 * mi:128 * mi + 128],
                             start=(ti == 0), stop=(ti == len(terms) - 1))
                P.cp(plT, pp[0:64, :], eng="act")
                pq = rA2.next()
                for g in range(4):
                    P.mm(pq[:, 64 * g:64 * g + 64], plT[:, 128 * g:128 * g + 128], wpl[:, 64 * g:64 * g + 64], start=True, stop=True)
                P.tt(Yt[:, 768:1024], pq[:, 0:256], psc, ALU.mult)
                for c in (range(3, 8) if part == 0 else range(8)):
                    P.tr(ptY[:, 128 * c:128 * c + 128], Yt[:, 128 * c:128 * c + 128], ident)
                P.cp(YTs, ptY, eng="act")

            def phC(b, i):
                for ng in range(2):
                    acc = None
                    for br, (c0, c1) in enumerate(((0, 3), (3, 6), (6, 8))):
                        p = rA2.next()
                        for c in range(c0, c1):
                            P.mm(p, YTs[:, 128 * c:128 * c + 128], Wbr[:, c * D + 512 * ng:c * D + 512 * ng + 512], start=(c == c0), stop=(c == c1 - 1))
                        gsl = gtB[:, D * br + 512 * ng:D * br + 512 * ng + 512]
                        if br == 0:
                            acc = e1.next()
                            P.tt(acc, p, gsl, ALU.mult)
                        else:
                            tmp = e1.next()
                            P.tt(tmp, p, gsl, ALU.mult)
                            if br == 1:
                                acc2 = e1.next()
                                P.tt(acc2, acc, tmp, ALU.add)
                                acc = acc2
                            else:
                                P.tt(mb[:, 512 * ng:512 * ng + 512], acc, tmp, ALU.add)

            def phD(b, i, xt_):
                pt = psT[1]
                for k in range(8):
                    P.tr(pt[:, 128 * k:128 * k + 128], mb[:, 128 * k:128 * k + 128], ident)
                P.cp(mTs, pt, eng="act")
                for ng in range(2):
                    p = rA2.next()
                    for k in range(8):
                        P.mm(p, mTs[:, 128 * k:128 * k + 128], Wout[:, k * D + 512 * ng:k * D + 512 * ng + 512], start=(k == 0), stop=(k == 7))
                    tmp = e1.next()
                    P.tt(tmp, p, GXt[:, 512 * ng:512 * ng + 512], ALU.mult)
                    P.tt(xt_[:, 512 * ng:512 * ng + 512], xt_[:, 512 * ng:512 * ng + 512], tmp, ALU.add)
                P.dma(xres[b][128 * i:128 * i + 128, :], xt_, concurrent=True)

            for b in range(bl):
                P.dma(kTs.rr("d (j t) -> d j t", j=8), V(qkT[b].ap[768:1280, :].rearrange("(j d) t -> d j t", d=64), qkT[b].reg))
                P.dma(vS.rr("p (i f) -> p i f", f=520), V(vsc[b].ap.rearrange("(i p) f -> p i f", p=128), vsc[b].reg))
                P.dma(pcS.rr("p (i f) -> p i f", f=256), V(pcs[b].ap.rearrange("(i p) f -> p i f", p=128), pcs[b].reg))
                for part in range(2):
                    if part == 1 and last:
                        continue
                    slot = b if part == 0 else 4
                    load_mod(l, slot, 2 * D, D, GXt)
                    pend = None
                    for i in (range(16) if part == 0 else range(16, 18)):
                        xt_ = phA(b, part, i)
                        if pend is not None:
                            phC(b, pend[0])
                        P.dma(gtB, gts[b][128 * i:128 * i + 128, :])
                        phB(b, part, i)
                        if pend is not None:
                            phD(b, *pend)
                        pend = (i, xt_)
                    phC(b, pend[0])
                    phD(b, *pend)

            cur_fence[0] = fence()
            W1 = carve(AR, "W1", 0, 128, 0, 8 * DFF)
            W3 = carve(AR, "W3", 0, 128, 8 * DFF, 8 * DFF)
            W2 = carve(AR, "W2", 0, 128, 16 * DFF, 22 * D)
            gg = carve(WKB, "gg", 0, 128, 0, DFF)
            gT = carve(WKB, "gT", 0, 128, DFF, 22 * 128)
            rUT = Rot([uT, carve(WKB, "uT2", 0, 128, 2 * DFF, D)])
            GXt = carve(WKF, "GXt", 0, 128, 0, D)
            load_mat(W1, w_ff1[l], 8, DFF)
            load_mat(W3, w_ff3[l], 8, DFF)
            load_mat(W2, w_ff2[l], 22, D)
            for b in range(bl):
                for part in range(2):
                    if part == 1 and last:
                        continue
                    slot = b if part == 0 else 4
                    load_mod(l, slot, 3 * D, 2 * D, modt)
                    load_mod(l, slot, 5 * D, D, GXt)
                    for i in (range(16) if part == 0 else range(16, 18)):
                        xt_ = rX.next()
                        P.dma(xt_, xres[b][128 * i:128 * i + 128, :])
                        uT_ = norm_mod(xt_, rUT)
                        for g in range(6):
                            c0 = 512 * g
                            n = min(512, DFF - c0)
                            p1 = rA4.next()
                            p3 = rA4.next()
                            for k in range(8):
                                P.mm(p1[:, 0:n], uT_[:, 128 * k:128 * k + 128], W1[:, k * DFF + c0:k * DFF + c0 + n], start=(k == 0), stop=(k == 7))
                            for k in range(8):
                                P.mm(p3[:, 0:n], uT_[:, 128 * k:128 * k + 128], W3[:, k * DFF + c0:k * DFF + c0 + n], start=(k == 0), stop=(k == 7))
                            e = e1.next()
                            P.act(e[:, 0:n], p1[:, 0:n], AF.Silu)
                            P.tt(gg[:, c0:c0 + n], p3[:, 0:n], e[:, 0:n], ALU.mult)
                        for j in range(3):
                            pt = rT.next()
                            nk = 8 if j < 2 else 6
                            for kk in range(nk):
                                k = 8 * j + kk
                                P.tr(pt[:, 128 * kk:128 * kk + 128], gg[:, 128 * k:128 * k + 128], ident)
                            P.cp(gT[:, 1024 * j:1024 * j + 128 * nk], pt[:, 0:128 * nk], eng="act")
                        for ng in range(2):
                            po = rA4.next()
                            for k in range(22):
                                P.mm(po, gT[:, 128 * k:128 * k + 128], W2[:, k * D + 512 * ng:k * D + 512 * ng + 512], start=(k == 0), stop=(k == 21))
                            e = e1.next()
                            P.tt(e, po, GXt[:, 512 * ng:512 * ng + 512], ALU.mult)
                            P.tt(xt_[:, 512 * ng:512 * ng + 512], xt_[:, 512 * ng:512 * ng + 512], e, ALU.add)
                        if last:
                            finals.append(P.dma(out[b, 128 * i:128 * i + 128, :], xt_, concurrent=True))
                        else:
                            P.dma(xres[b][128 * i:128 * i + 128, :], xt_, concurrent=True)
        P.finalize(st, final_waits=finals)
    return nc


def _band_matrices():
    N = 384
    out = np.zeros((128, 20, 128), np.float32)
    t = np.arange(N)
    for g, win in enumerate(POOL_WINDOWS):
        lo = np.clip(t - win // 2, 0, N)
        hi = np.clip(t + win // 2, 0, N)
        M = np.zeros((N, N), np.float32)
        for tt_ in range(N):
            M[tt_, lo[tt_]:hi[tt_]] = 1.0 / float(hi[tt_] - lo[tt_])
            M[tt_, tt_] -= 1.0
        out[:, 5 * g + 0, :] = M[0:128, 0:128].T
        out[:, 5 * g + 1, :] = M[128:256, 128:256].T
        out[:, 5 * g + 2, :] = M[256:384, 256:384].T
        out[:, 5 * g + 3, :] = M[128:256, 0:128].T
        out[:, 5 * g + 4, :] = M[0:128, 128:256].T
    return out.reshape(128, 20 * 128)


def _rope_tables():
    t = np.arange(S)
    row = (t // GRID_W).astype(np.float32)
    col = (t % GRID_W).astype(np.float32)
    freqs = (10000.0 ** (-np.arange(16, dtype=np.float32) / 16)).astype(np.float32)
    ang = np.concatenate([row[:, None] * freqs[None, :], col[:, None] * freqs[None, :]], axis=1)
    c = np.cos(ang).astype(np.float32).reshape(16, 128, 32).transpose(1, 0, 2).reshape(128, 512)
    s = np.sin(ang).astype(np.float32).reshape(16, 128, 32).transpose(1, 0, 2).reshape(128, 512)
    return np.ascontiguousarray(c), np.ascontiguousarray(s)


_PROGS = {}
IN_PERM = np.concatenate([np.arange(0, 384), np.arange(1152, 1536), np.arange(384, 768), np.arange(1536, 1664),
                          np.arange(1792, 2048), np.arange(768, 1152), np.arange(1664, 1792), np.arange(2048, 5120)])


def prep_shared(w_mod, b_mod, norm1_g, norm2_g, w_in, q_norm_a, k_norm_a, q_norm_b, k_norm_b, rpb_a, w_pool,
                pool_scale, w_br_a, w_br_b, w_br_c, w_out, w_ff1, w_ff3, w_ff2, depth=DEPTH):
    f = lambda a: np.ascontiguousarray(np.asarray(a, dtype=np.float32)[:depth])
    rc, rs = _rope_tables()
    cq = np.arange(GRID_W)
    col_start = np.clip(cq - 8, 0, GRID_W - 16)
    col_valid = (cq[None, :] >= col_start[:, None]) & (cq[None, :] < col_start[:, None] + 16)
    dc_idx = np.clip(cq[None, :] - cq[:, None] + 15, 0, 30)
    rpb = f(rpb_a)
    G = np.empty((depth, 2, 64, 6, 14, 64), np.float32)
    for j in range(2):
        for dr0 in range(14):
            G[:, j, :, :, dr0, :] = rpb[:, :, dr0 + j, :][:, :, dc_idx].transpose(0, 3, 1, 2)
    cm = np.ascontiguousarray(np.tile(col_valid.T.astype(np.float32), (2, 1)))
    qn_a, kn_a, qn_b, kn_b = f(q_norm_a), f(k_norm_a), f(q_norm_b), f(k_norm_b)
    qkg = np.concatenate([np.tile(qn_a, (1, 6)), np.tile(qn_b, (1, 6)), np.tile(kn_a, (1, 6)), np.tile(kn_b, (1, 2))], axis=1)
    return {
        "w_mod": f(w_mod), "b_mod": f(b_mod),
        "ng12": np.ascontiguousarray(np.concatenate([f(norm1_g), f(norm2_g)], axis=1)),
        "w_in": np.ascontiguousarray(f(w_in)[:, :, IN_PERM]),
        "qkg": np.ascontiguousarray(qkg), "ropec": rc, "ropes": rs,
        "erpb": np.ascontiguousarray(G.reshape(depth, 128, 6 * 14 * 64)), "cmask": cm,
        "bandm": _band_matrices(),
        "wpool": np.ascontiguousarray(f(w_pool).transpose(0, 2, 1, 3).reshape(depth, 64, 256)),
        "pscale": f(pool_scale),
        "w_br": np.ascontiguousarray(np.concatenate([f(w_br_a), f(w_br_b), f(w_br_c)], axis=1)),
        "w_out": f(w_out), "w_ff1": f(w_ff1), "w_ff3": f(w_ff3), "w_ff2": f(w_ff2),
        "ident": np.eye(128, dtype=np.float32),
    }


def prep_core(x, c, ctx, c_ctx, b0, bl):
    xin = np.concatenate([x[b0:b0 + bl], ctx[b0:b0 + bl]], axis=1)
    cv = np.zeros((5, D), np.float32)
    cv[0:bl] = c[b0:b0 + bl]
    cv[4] = c_ctx
    cT = np.ascontiguousarray(cv.reshape(5, 8, 128).transpose(2, 1, 0)).reshape(128, 40)
    return {"xin": np.ascontiguousarray(xin), "cT": cT}


def kernel(x, c, ctx, c_ctx, w_mod, b_mod, norm1_g, norm2_g, w_in, q_norm_a, k_norm_a,
           q_norm_b, k_norm_b, rpb_a, w_pool, pool_scale, w_br_a, w_br_b, w_br_c, w_out,
           w_ff1, w_ff3, w_ff2):
    f = lambda a: np.ascontiguousarray(np.asarray(a, dtype=np.float32))
    x, c, ctx, c_ctx = f(x), f(c), f(ctx), f(c_ctx)
    if (DEPTH, BL) not in _PROGS:
        _PROGS[(DEPTH, BL)] = build_program(DEPTH, BL)
    nc = _PROGS[(DEPTH, BL)]
    shared = prep_shared(w_mod, b_mod, norm1_g, norm2_g, w_in, q_norm_a, k_norm_a, q_norm_b, k_norm_b, rpb_a, w_pool,
                         pool_scale, w_br_a, w_br_b, w_br_c, w_out, w_ff1, w_ff3, w_ff2)
    in_maps = []
    for core in range(NCORES):
        m = prep_core(x, c, ctx, c_ctx, core * BL, BL)
        m.update(shared)
        in_maps.append(m)
    res = run_bass_kernel_spmd(nc, in_maps, core_ids=list(range(NCORES)))
    return np.concatenate([np.asarray(r["out"], dtype=np.float32) for r in res.results], axis=0)
```

```python
import numpy as np
from contextlib import ExitStack
import concourse.bass as bass
import concourse.mybir as mybir
from concourse.bass_utils import run_bass_kernel_spmd

F32 = mybir.dt.float32
BF16 = mybir.dt.bfloat16
AF = mybir.ActivationFunctionType
ALU = mybir.AluOpType
AX = mybir.AxisListType

NCORES = 8
BL = 4
S = 2048
LC = 256
NT = S + LC
D = 1024
DFF = 2816
DEPTH = 4
EPS = 1e-6
ENGS = ("pe", "act", "dve", "pool", "sp")


class Reg:
    __slots__ = ("name", "ws", "rs", "war")

    def __init__(self, name):
        self.name = name
        self.ws = []
        self.rs = []
        self.war = []


class Op:
    __slots__ = ("eng", "emit", "deps", "marked", "ev", "dma")

    def __init__(self, eng, emit, dma=None):
        self.eng = eng
        self.emit = emit
        self.deps = []
        self.marked = False
        self.ev = None
        self.dma = dma


class V:
    __slots__ = ("ap", "reg")

    def __init__(self, ap, reg):
        self.ap = ap
        self.reg = reg

    def __getitem__(self, k):
        return V(self.ap[k], self.reg)

    def rr(self, pat, **kw):
        return V(self.ap.rearrange(pat, **kw), self.reg)

    def bc(self, axis, shape):
        return V(self.ap.unsqueeze(axis).to_broadcast(shape), self.reg)


class Prog:
    def __init__(self, nc):
        self.nc = nc
        self.ops = {e: [] for e in ENGS}

    def add(self, eng, emit, reads=(), writes=(), dma=None, concurrent=False):
        op = Op(eng, emit, dma)
        deps = op.deps
        for r in reads:
            deps.extend(r.ws)
            r.rs.append(op)
        for w in writes:
            if concurrent:
                if w.rs:
                    w.war = w.rs
                    w.rs = []
                    w.ws = []
                deps.extend(w.war)
                w.ws.append(op)
            else:
                deps.extend(w.ws)
                deps.extend(w.rs)
                deps.extend(w.war)
                w.ws = [op]
                w.rs = []
                w.war = []
        op.deps = [d for d in deps if d is not op]
        if eng == "pe":
            op.deps = [d for d in op.deps if not (d.eng == "pe" and d.dma is None)]
        self.ops[eng].append(op)
        return op

    def dma(self, out, in_, key=None, concurrent=False, q=None):
        k = key or out.reg.name
        if q is None:
            q = "sp" if out.reg.name.startswith("sb:") else "pool"
        return self.add(q, lambda e: e.dma_start(out=out.ap, in_=in_.ap), [in_.reg], [out.reg], dma=k, concurrent=concurrent)

    def mm(self, out, lhsT, rhs, start, stop):
        return self.add("pe", lambda e: e.matmul(out.ap, lhsT.ap, rhs.ap, start=start, stop=stop), [lhsT.reg, rhs.reg], [out.reg])

    def tr(self, out, in_, ident):
        return self.add("pe", lambda e: e.transpose(out=out.ap, in_=in_.ap, identity=ident.ap), [in_.reg, ident.reg], [out.reg])

    def act(self, out, in_, func, scale=1.0):
        return self.add("act", lambda e: e.activation(out=out.ap, in_=in_.ap, func=func, scale=scale), [in_.reg], [out.reg])

    def bcv(self, v, axis, shape):
        return V(v.ap.unsqueeze(axis).to_broadcast(shape), v.reg)

    def tt(self, out, a, b, op, eng="dve"):
        return self.add(eng, lambda e: e.tensor_tensor(out=out.ap, in0=a.ap, in1=b.ap, op=op), [a.reg, b.reg], [out.reg])

    def ts(self, out, a, s1, s2, op0=ALU.mult, op1=ALU.add):
        return self.add("dve", lambda e: e.tensor_scalar(out=out.ap, in0=a.ap, scalar1=s1, scalar2=s2, op0=op0, op1=op1), [a.reg], [out.reg])

    def stt(self, out, a, sc, b, op0, op1):
        return self.add("dve", lambda e: e.scalar_tensor_tensor(out=out.ap, in0=a.ap, scalar=sc.ap, in1=b.ap, op0=op0, op1=op1),
                        [a.reg, sc.reg, b.reg], [out.reg])

    def cp(self, out, in_, eng="dve"):
        if eng == "act":
            return self.act(out, in_, AF.Copy)
        return self.add("dve", lambda e: e.tensor_copy(out=out.ap, in_=in_.ap), [in_.reg], [out.reg])

    def red(self, out, in_):
        return self.add("dve", lambda e: e.tensor_reduce(out=out.ap, in_=in_.ap, axis=AX.X, op=ALU.add), [in_.reg], [out.reg])

    def rcp(self, out, in_):
        return self.add("dve", lambda e: e.reciprocal(out=out.ap, in_=in_.ap), [in_.reg], [out.reg])

    def memset(self, out, val):
        return self.add("dve", lambda e: e.memset(out.ap, val), [], [out.reg])

    def finalize(self, stack, final_waits=()):
        nc = self.nc
        esem = {e: stack.enter_context(nc.semaphore("s_" + e)) for e in ENGS}
        for e in ENGS:
            for op in self.ops[e]:
                for d in op.deps:
                    d.marked = True
        for op in final_waits:
            op.marked = True
        dsem = {}
        for e in ENGS:
            c = 0
            for op in self.ops[e]:
                if op.dma is not None:
                    if op.dma not in dsem:
                        dsem[op.dma] = [stack.enter_context(nc.semaphore("d_%d" % len(dsem))), 0]
                    s = dsem[op.dma]
                    s[1] += 16
                    op.ev = (s[0], s[1], op.dma)
                elif op.marked:
                    c += 1
                    op.ev = (esem[e], c, e)
        block = stack.enter_context(nc.Block())
        ops = self.ops

        def run(engname, eng, extra=None):
            seen = {}
            for op in ops[engname]:
                need = {}
                for d in op.deps:
                    s, v, k = d.ev
                    if seen.get(k, 0) >= v:
                        continue
                    if k not in need or need[k][1] < v:
                        need[k] = (s, v)
                for k, (s, v) in need.items():
                    eng.wait_ge(s, v)
                    seen[k] = v
                ins = op.emit(eng)
                if op.dma is not None:
                    ins.then_inc(op.ev[0], 16)
                elif op.marked:
                    ins.then_inc(op.ev[0], 1)
            if extra:
                for op in extra:
                    s, v, k = op.ev
                    if seen.get(k, 0) < v:
                        eng.wait_ge(s, v)
                        seen[k] = v

        @block.tensor
        def _(eng):
            run("pe", eng)

        @block.scalar
        def _(eng):
            run("act", eng)

        @block.vector
        def _(eng):
            run("dve", eng)

        @block.gpsimd
        def _(eng):
            run("pool", eng)

        @block.sync
        def _(eng):
            run("sp", eng, extra=final_waits)


class Rot:
    def __init__(self, items):
        self.items = items
        self.i = 0

    def next(self):
        v = self.items[self.i % len(self.items)]
        self.i += 1
        return v


GRID_W = 64
QKW = 1280
INW = 5120
POOL_WINDOWS = (2, 4, 8, 16)


def build_program(depth=DEPTH, bl=BL):
    nc = bass.Bass("TRN2", target_bir_lowering=False)

    def din(name, shape):
        return V(nc.dram_tensor(name, list(shape), F32, kind="ExternalInput").ap(), Reg("in:" + name))

    def dscr(name, shape, dt):
        return V(nc.dram_tensor(name, list(shape), dt).ap(), Reg("dr:" + name))

    xin = din("xin", [bl, NT, D])
    cT = din("cT", [128, 40])
    w_mod = din("w_mod", [depth, D, 6 * D])
    b_mod = din("b_mod", [depth, 6 * D])
    ng12 = din("ng12", [depth, 2 * D])
    w_in = din("w_in", [depth, D, INW])
    qkg = din("qkg", [depth, QKW])
    ropec = din("ropec", [128, 16 * 32])
    ropes = din("ropes", [128, 16 * 32])
    erpb = din("erpb", [depth, 128, 6 * 14 * 64])
    cmask = din("cmask", [128, 64])
    bandm = din("bandm", [128, 20 * 128])
    wpool = din("wpool", [depth, 64, 256])
    pscale = din("pscale", [depth, 256])
    w_br = din("w_br", [depth, D, D])
    w_out = din("w_out", [depth, D, D])
    w_ff1 = din("w_ff1", [depth, D, DFF])
    w_ff3 = din("w_ff3", [depth, D, DFF])
    w_ff2 = din("w_ff2", [depth, DFF, D])
    ident_d = din("ident", [128, 128])
    out = V(nc.dram_tensor("out", [bl, S, D], F32, kind="ExternalOutput").ap(), Reg("dr:out"))
    xres = [dscr("xres%d" % b, [NT, D], F32) for b in range(bl)]
    qkT = [dscr("qkT%d" % b, [QKW, NT], BF16) for b in range(bl)]
    vsc = [dscr("vsc%d" % b, [NT, 520], BF16) for b in range(bl)]
    pcs = [dscr("pcs%d" % b, [NT, 256], BF16) for b in range(bl)]
    gts = [dscr("gts%d" % b, [NT, 3 * D], BF16) for b in range(bl)]
    modd = [dscr("modd%d" % l, [5, 6 * D], F32) for l in range(depth)]

    with ExitStack() as st:
        def sbt(name, shape, dt):
            return st.enter_context(nc.sbuf_tensor("s_" + name, list(shape), dt))

        def sb(name, shape, dt):
            return V(sbt(name, shape, dt)[:], Reg("sb:" + name))

        def ps(name, shape, dt):
            return V(st.enter_context(nc.psum_tensor("p_" + name, list(shape), dt))[:], Reg("ps:" + name))

        P = Prog(nc)
        ARN = 67584
        AR = sbt("AR", [128, ARN], BF16)
        WKF = sbt("WKF", [128, 4608], F32)
        WKB = sbt("WKB", [128, 9216], BF16)
        live = []

        def fence():
            ops = []
            for r in live:
                ops.extend(r.ws)
                ops.extend(r.rs)
                ops.extend(r.war)
            del live[:]
            return ops

        cur_fence = [[]]

        def carve(base, name, p0, p1, o0, n):
            r = Reg("sb:" + name)
            r.war = list(cur_fence[0])
            live.append(r)
            return V(base[p0:p1, o0:o0 + n], r)

        stg = Rot([sb("stg%d" % i, [128, 1024], F32) for i in range(2)])
        modt = sb("modt", [128, 2 * D], F32)
        xt = sb("xt", [128, D], F32)
        t1 = sb("t1", [128, D], F32)
        u = sb("u", [128, D], BF16)
        uT = sb("uT", [128, D], BF16)
        e1 = Rot([sb("e1_%d" % i, [128, 512], F32) for i in range(3)])
        sm = sb("sm", [128, 32], F32)
        ident = sb("ident", [128, 128], BF16)
        csf = sb("csf", [128, 40], F32)
        cse = sb("cse", [128, 40], F32)
        scT = sb("scT", [128, 8 * 32], BF16)
        psA = [ps("psA%d" % i, [128, 512], F32) for i in range(2)]
        psS = [ps("psS%d" % i, [128, 512], F32) for i in range(2)]
        pY = ps("pY", [128, 512], F32)
        pN = ps("pN", [128, 512], F32)
        psT = [ps("psT%d" % i, [128, 1024], BF16) for i in range(2)]
        rA4 = Rot(psA + psS)
        rA2 = Rot(psA)
        rS = Rot(psS)
        rT = Rot(psT)
        finals = []
        rX = Rot([xt, t1])

        s_ = stg.next()
        P.dma(s_[:, 0:128], ident_d)
        P.cp(ident, s_[:, 0:128])
        P.dma(csf, cT)
        P.act(cse, csf, AF.Sigmoid)
        P.memset(scT, 0.0)
        P.tt(scT.rr("p (k j) -> p k j", j=32)[:, :, 0:5], csf.rr("p (k j) -> p k j", j=5), cse.rr("p (k j) -> p k j", j=5), ALU.mult)

        cast_i = [0]

        def load_cast(dst, src, n):
            s_ = stg.next()
            P.dma(s_[0:dst.ap.shape[0], 0:n], src)
            P.cp(dst, s_[0:dst.ap.shape[0], 0:n], eng=("act" if cast_i[0] % 2 else "dve"))
            cast_i[0] += 1

        def load_mat(dst, src2d, nk, ncol):
            for k in range(nk):
                c0 = 0
                while c0 < ncol:
                    n = min(1024, ncol - c0)
                    load_cast(dst[:, k * ncol + c0:k * ncol + c0 + n], src2d[128 * k:128 * k + 128, c0:c0 + n], n)
                    c0 += n

        def norm_a(x_v):
            for hf in range(2):
                e = e1.next()
                P.tt(e, x_v[:, 512 * hf:512 * hf + 512], x_v[:, 512 * hf:512 * hf + 512], ALU.mult)
                P.red(sm[:, hf:hf + 1], e.rr("p (o f) -> p o f", o=1))
            P.tt(sm[:, 2:3], sm[:, 0:1], sm[:, 1:2], ALU.add)
            P.ts(sm[:, 3:4], sm[:, 2:3], 1.0 / D, EPS)
            P.act(sm[:, 4:5], sm[:, 3:4], AF.Sqrt)
            P.rcp(sm[:, 5:6], sm[:, 4:5])
            for hf in range(2):
                e = e1.next()
                P.stt(e, x_v[:, 512 * hf:512 * hf + 512], sm[:, 5:6], modt[:, D + 512 * hf:D + 512 * hf + 512], ALU.mult, ALU.mult)
                P.tt(u[:, 512 * hf:512 * hf + 512], e, modt[:, 512 * hf:512 * hf + 512], ALU.add)

        def norm_b(rUT):
            pt = rT.next()
            for k in range(8):
                P.tr(pt[:, 128 * k:128 * k + 128], u[:, 128 * k:128 * k + 128], ident)
            uT_ = rUT.next()
            P.cp(uT_, pt, eng="act")
            return uT_

        def load_mod(l, slot, c0, n, dst):
            P.dma(dst, V(modd[l].ap[slot:slot + 1, c0:c0 + n].partition_broadcast(128), modd[l].reg))

        for l in range(depth):
            last = l == depth - 1
            xsrc = (lambda b: xin[b]) if l == 0 else (lambda b: xres[b])

            cur_fence[0] = fence()
            msec = carve(WKF, "msec", 0, 5, 0, D)
            bsec = carve(WKF, "bsec", 0, 5, D, D)
            g12 = carve(WKF, "g12", 0, 5, 2 * D, 2 * D)
            wm = Rot([carve(WKB, "wm%d" % i, 0, 128, 512 * i, 512) for i in range(2)])
            P.dma(g12, V(ng12.ap[l:l + 1, :].partition_broadcast(5), ng12.reg))
            for sec in range(6):
                c0 = sec * D
                P.dma(bsec, V(b_mod.ap[l:l + 1, c0:c0 + D].partition_broadcast(5), b_mod.reg))
                for g in range(2):
                    pm = rA2.next()
                    for k in range(8):
                        w = wm.next()
                        s_ = stg.next()
                        P.dma(s_[:, 0:512], w_mod[l, 128 * k:128 * k + 128, c0 + 512 * g:c0 + 512 * g + 512])
                        P.cp(w, s_[:, 0:512], eng=("act" if k % 2 else "dve"))
                        P.mm(pm[0:32, :], scT[:, 32 * k:32 * k + 32], w, start=(k == 0), stop=(k == 7))
                    P.tt(msec[:, 512 * g:512 * g + 512], pm[0:5, :], bsec[:, 512 * g:512 * g + 512], ALU.add)
                if sec in (1, 4):
                    gg_ = g12[:, 0:D] if sec == 1 else g12[:, D:2 * D]
                    P.ts(msec, msec, 1.0, 1.0)
                    P.tt(msec, msec, gg_, ALU.mult)
                P.dma(modd[l][:, sec * D:(sec + 1) * D], msec, concurrent=True)

            cur_fence[0] = fence()
            Win = carve(AR, "Win", 0, 128, 0, 8 * INW)
            zf = carve(WKF, "zf", 0, 128, 0, 512)
            sq = carve(WKF, "sq", 0, 128, 512, 512)
            qnf = carve(WKF, "qnf", 0, 128, 1024, QKW)
            QKG = carve(WKF, "QKG", 0, 128, 2304, QKW)
            RC = carve(WKF, "RC", 0, 128, 3584, 512)
            RS = carve(WKF, "RS", 0, 128, 4096, 512)
            qnb = carve(WKB, "qnb", 0, 128, 0, QKW)
            qkTs = carve(WKB, "qkTs", 0, 128, 1280, QKW)
            vt = carve(WKB, "vt", 0, 128, 2560, 520)
            pct = carve(WKB, "pct", 0, 128, 3080, 256)
            gt = carve(WKB, "gt", 0, 128, 3336, 3 * D)
            rUT = Rot([uT, carve(WKB, "uT2", 0, 128, 6600, D)])
            load_mat(Win, w_in[l], 8, INW)
            P.dma(QKG, V(qkg.ap[l:l + 1, :].partition_broadcast(128), qkg.reg))
            P.dma(RC, ropec)
            P.dma(RS, ropes)
            P.memset(vt, 1.0)
            def a_S1(b, part, i, uT_):
                for g in range(10):
                    p = rA4.next()
                    for k in range(8):
                        P.mm(p, uT_[:, 128 * k:128 * k + 128], Win[:, k * INW + 512 * g:k * INW + 512 * g + 512], start=(k == 0), stop=(k == 7))
                    if g < 3:
                        n = 512 if g < 2 else 256
                        nh = n // 64
                        P.cp(zf[:, 0:n], p[:, 0:n], eng="act")
                        if g == 2:
                            P.cp(pct, p[:, 256:512], eng="act")
                            P.dma(pcs[b][128 * i:128 * i + 128, :], pct, concurrent=True)
                        P.tt(sq[:, 0:n], zf[:, 0:n], zf[:, 0:n], ALU.mult)
                        P.red(sm[:, 8:8 + nh], sq[:, 0:n].rr("p (h d) -> p h d", d=64))
                        P.ts(sm[:, 8:8 + nh], sm[:, 8:8 + nh], 1.0 / 64, EPS)
                        P.act(sm[:, 16:16 + nh], sm[:, 8:8 + nh], AF.Sqrt)
                        P.rcp(sm[:, 24:24 + nh], sm[:, 16:16 + nh])
                        P.tt(sq[:, 0:n].rr("p (h d) -> p h d", d=64), zf[:, 0:n].rr("p (h d) -> p h d", d=64),
                             P.bcv(sm[:, 24:24 + nh], 2, [128, nh, 64]), ALU.mult)
                        P.tt(qnf[:, 512 * g:512 * g + n], sq[:, 0:n], QKG[:, 512 * g:512 * g + n], ALU.mult)
                    elif g == 3:
                        P.cp(vt.rr("p (h d) -> p h d", d=65)[:, :, 0:64], p.rr("p (h d) -> p h d", d=64), eng="act")
                        P.dma(vsc[b][128 * i:128 * i + 128, :], vt, concurrent=True)
                    else:
                        P.act(gt[:, 512 * (g - 4):512 * (g - 4) + 512], p, AF.Sigmoid)
                P.dma(gts[b][128 * i:128 * i + 128, :], gt, concurrent=True)

            def a_rope(part, i):
                P.cp(qnb, qnf)
                if part == 0:
                    cosv = RC[:, 32 * i:32 * i + 32].rr("p (a f) -> p a f", a=2)
                    sinv = RS[:, 32 * i:32 * i + 32].rr("p (a f) -> p a f", a=2)
                    for (c0, H) in ((384, 6), (1152, 2)):
                        shp = [128, H, 2, 16]
                        X = qnf[:, c0:c0 + 64 * H].rr("p (h a t f) -> p h a t f", a=2, t=2, f=16)
                        O = qnb[:, c0:c0 + 64 * H].rr("p (h a t f) -> p h a t f", a=2, t=2, f=16)
                        x1, x2 = X[:, :, :, 0, :], X[:, :, :, 1, :]
                        cb = V(cosv.ap.unsqueeze(1).to_broadcast(shp), cosv.reg)
                        sbv = V(sinv.ap.unsqueeze(1).to_broadcast(shp), sinv.reg)
                        ta = zf[:, 0:32 * H].rr("p (h a f) -> p h a f", a=2, f=16)
                        tb = sq[:, 0:32 * H].rr("p (h a f) -> p h a f", a=2, f=16)
                        P.tt(ta, x1, cb, ALU.mult)
                        P.tt(tb, x2, sbv, ALU.mult)
                        P.tt(O[:, :, :, 0, :], ta, tb, ALU.subtract)
                        P.tt(ta, x2, cb, ALU.mult)
                        P.tt(tb, x1, sbv, ALU.mult)
                        P.tt(O[:, :, :, 1, :], ta, tb, ALU.add)

            def a_tr(b, i):
                pt = rT.next()
                for c in range(8):
                    P.tr(pt[:, 128 * c:128 * c + 128], qnb[:, 128 * c:128 * c + 128], ident)
                P.cp(qkTs[:, 0:1024], pt, eng="act")
                pt = rT.next()
                for c in range(2):
                    P.tr(pt[:, 128 * c:128 * c + 128], qnb[:, 1024 + 128 * c:1024 + 128 * c + 128], ident)
                P.cp(qkTs[:, 1024:1280], pt[:, 0:256], eng="act")
                P.dma(V(qkT[b].ap[:, 128 * i:128 * i + 128].rearrange("(c p) t -> p c t", p=128), qkT[b].reg),
                      qkTs.rr("p (c t) -> p c t", t=128), concurrent=True)

            def a_S0a(b, i):
                xt_ = rX.next()
                P.dma(xt_, xsrc(b)[128 * i:128 * i + 128, :])
                norm_a(xt_)

            for b in range(bl):
                for part in range(2):
                    slot = b if part == 0 else 4
                    load_mod(l, slot, 0, 2 * D, modt)
                    tl = list(range(16) if part == 0 else range(16, 18))
                    a_S0a(b, tl[0])
                    uT_ = norm_b(rUT)
                    pend = None
                    for ix, i in enumerate(tl):
                        nxt_ = tl[ix + 1] if ix + 1 < len(tl) else None
                        if nxt_ is not None:
                            a_S0a(b, nxt_)
                        a_S1(b, part, i, uT_)
                        if nxt_ is not None:
                            uT_ = norm_b(rUT)
                        if pend is not None:
                            a_tr(b, pend)
                        a_rope(part, i)
                        pend = i
                    a_tr(b, pend)

            cur_fence[0] = fence()
            o = 0
            Wbr = carve(AR, "Wbr", 0, 128, o, 8 * D); o += 8 * D
            Wout = carve(AR, "Wout", 0, 128, o, 8 * D); o += 8 * D
            kTs = carve(AR, "kTs", 0, 64, o, 8 * NT); o += 8 * NT
            vS = carve(AR, "vS", 0, 128, o, 18 * 520); o += 18 * 520
            pcS = carve(AR, "pcS", 0, 128, o, 18 * 256); o += 18 * 256
            Et = carve(AR, "Et", 0, 128, o, 6 * 14 * 64); o += 6 * 14 * 64
            MT = carve(AR, "MT", 0, 128, o, 20 * 128); o += 20 * 128
            wpl = carve(AR, "wpl", 0, 64, o, 256); o += 256
            pTb = [carve(AR, "pT%d" % i, 0, 128, o + 2304 * i, 2304) for i in range(2)]; o += 4608
            vwin = [carve(AR, "vwin%d" % i, 0, 128, o + 1560 * i, 1560) for i in range(2)]; o += 3120
            qTs = carve(AR, "qTs", 0, 64, o, 12 * 128); o += 12 * 128
            assert o <= ARN
            psc = carve(WKF, "psc", 0, 128, 0, 256)
            cmk = carve(WKF, "cmk", 0, 128, 256, 64)
            GXt = carve(WKF, "GXt", 0, 128, 512, D)
            o = 0
            gtB = carve(WKB, "gtB", 0, 128, o, 3 * D); o += 3 * D
            Yt = carve(WKB, "Yt", 0, 128, o, D); o += D
            YTs = carve(WKB, "YTs", 0, 128, o, D); o += D
            mb = carve(WKB, "mb", 0, 128, o, D); o += D
            mTs = carve(WKB, "mTs", 0, 128, o, D); o += D
            pnb = [carve(WKB, "pn%d" % i, 0, 128, o + 384 * i, 384) for i in range(2)]; o += 768
            yhb = [carve(WKB, "yh%d" % i, 0, 64, o + 384 * i, 384) for i in range(2)]; o += 768
            plT = carve(WKB, "plT", 0, 64, o, 512); o += 512
            assert o <= 9216
            rpn, ryh, rpT, rvw = Rot(pnb), Rot(yhb), Rot(pTb), Rot(vwin)
            load_mat(Wbr, w_br[l], 8, D)
            load_mat(Wout, w_out[l], 8, D)
            load_cast(wpl, wpool[l], 256)
            for j in range(3):
                n = min(1024, 2560 - 1024 * j)
                load_cast(MT[:, 1024 * j:1024 * j + n], bandm[:, 1024 * j:1024 * j + n], n)
            P.dma(psc, V(pscale.ap[l:l + 1, :].partition_broadcast(128), pscale.reg))
            P.dma(cmk, cmask)
            for j in range(6):
                s_ = stg.next()
                P.dma(s_[:, 0:896], erpb[l, :, 896 * j:896 * j + 896])
                e_ = e1.next()
                for hh in range(2):
                    P.act(e_[:, 0:448], s_[:, 448 * hh:448 * hh + 448], AF.Exp)
                    P.tt(Et[:, 896 * j + 448 * hh:896 * j + 448 * hh + 448].rr("p (r q) -> p r q", q=64),
                         e_[:, 0:448].rr("p (r q) -> p r q", q=64), P.bcv(cmk, 1, [128, 7, 64]), ALU.mult)
            E4 = Et.rr("p (h r q) -> p h r q", h=6, r=14)

            def dense(qh0, kh0, voff, group, chunks, ycol0):
                nck = len(chunks)

                def scores(h):
                    kh = kh0 + h // group
                    pT = rpT.next()
                    for b0 in range(0, nck, 4):
                        blk = chunks[b0:b0 + 4]
                        p = rS.next()
                        for j, kc in enumerate(blk):
                            P.mm(p[:, 128 * j:128 * j + 128], kTs[:, kh * NT + 128 * kc:kh * NT + 128 * kc + 128], qTs[:, 128 * (qh0 + h):128 * (qh0 + h) + 128],
                                 start=True, stop=True)
                        P.act(pT[:, 128 * b0:128 * b0 + 128 * len(blk)], p[:, 0:128 * len(blk)], AF.Exp, scale=0.125)
                    return pT

                def pvm(h, pT):
                    vcol = voff + 65 * (h // group)
                    for idx, kc in enumerate(chunks):
                        P.mm(pY[:, 65 * h:65 * h + 65], pT[:, 128 * idx:128 * idx + 128], vS[:, 520 * kc + vcol:520 * kc + vcol + 65],
                             start=(idx == 0), stop=(idx == nck - 1))

                prev = None
                for h in range(6):
                    pT = scores(h)
                    if prev is not None:
                        pvm(*prev)
                    prev = (h, pT)
                pvm(*prev)
                pv = pY[:, 0:390].rr("p (h d) -> p h d", d=65)
                P.rcp(sm[:, 8:14], pv[:, :, 64])
                P.tt(Yt[:, ycol0:ycol0 + 384].rr("p (h d) -> p h d", d=64), pv[:, :, 0:64], P.bcv(sm[:, 8:14], 2, [128, 6, 64]), ALU.mult)

            for b in range(bl):
                P.dma(kTs.rr("d (j t) -> d j t", j=8), V(qkT[b].ap[768:1280, :].rearrange("(j d) t -> d j t", d=64), qkT[b].reg))
                P.dma(vS.rr("p (i f) -> p i f", f=520), V(vsc[b].ap.rearrange("(i p) f -> p i f", p=128), vsc[b].reg))
                P.dma(pcS.rr("p (i f) -> p i f", f=256), V(pcs[b].ap.rearrange("(i p) f -> p i f", p=128), pcs[b].reg))
                for part in range(2):
                    if part == 1 and last:
                        continue
                    slot = b if part == 0 else 4
                    load_mod(l, slot, 2 * D, D, GXt)
                    for i in (range(16) if part == 0 else range(16, 18)):
                        P.dma(qTs.rr("d (j t) -> d j t", j=12), V(qkT[b].ap[0:768, 128 * i:128 * i + 128].rearrange("(j d) t -> d j t", d=64), qkT[b].reg))
                        xt_ = rX.next()
                        P.dma(xt_, xsrc(b)[128 * i:128 * i + 128, :])
                        P.dma(gtB, gts[b][128 * i:128 * i + 128, :])
                        ptY = psT[0]
                        if part == 0:
                            dense(6, 6, 390, 3, list(range(18)), 384)
                            for half in range(2):
                                r = 2 * i + half
                                rs_ = min(max(r - 4, 0), 24)
                                off = rs_ - r + 7
                                tok0 = 64 * rs_
                                vw = rvw.next()
                                P.dma(vw.rr("p (c f) -> p c f", f=390), V(vsc[b].ap[tok0:tok0 + 512, 0:390].rearrange("(c p) f -> p c f", p=128), vsc[b].reg))
                                def na_scores(h):
                                    p = rS.next()
                                    for c in range(6):
                                        t0 = tok0 + 128 * c if c < 4 else S + 128 * (c - 4)
                                        P.mm(p[:, 64 * c:64 * c + 64], kTs[:, h * NT + t0:h * NT + t0 + 128], qTs[:, 128 * h + 64 * half:128 * h + 64 * half + 64],
                                             start=True, stop=True)
                                    pn = rpn.next()
                                    P.act(pn, p[:, 0:384], AF.Exp, scale=0.125)
                                    P.tt(pn[:, 0:256].rr("p (c q) -> p c q", q=64), pn[:, 0:256].rr("p (c q) -> p c q", q=64),
                                         V(E4.ap[:, h, off:off + 7:2, :], E4.reg), ALU.mult)
                                    return pn

                                def na_pv(h, pn):
                                    for c in range(6):
                                        rhs = vw[:, 390 * c + 65 * h:390 * c + 65 * h + 65] if c < 4 else vS[:, 520 * (16 + c - 4) + 65 * h:520 * (16 + c - 4) + 65 * h + 65]
                                        P.mm(pN[0:64, 65 * h:65 * h + 65], pn[:, 64 * c:64 * c + 64], rhs, start=(c == 0), stop=(c == 5))

                                prev = None
                                for h in range(6):
                                    pn = na_scores(h)
                                    if prev is not None:
                                        na_pv(*prev)
                                    prev = (h, pn)
                                na_pv(*prev)
                                pv = pN[0:64, 0:390].rr("p (h d) -> p h d", d=65)
                                P.rcp(sm[0:64, 16:22], pv[:, :, 64])
                                yh = ryh.next()
                                P.tt(yh.rr("p (h d) -> p h d", d=64), pv[:, :, 0:64], P.bcv(sm[0:64, 16:22], 2, [64, 6, 64]), ALU.mult)
                                for cc in range(3):
                                    P.tr(ptY[:, 128 * cc + 64 * half:128 * cc + 64 * half + 64], yh[:, 128 * cc:128 * cc + 128], ident[0:64, 0:64])
                        else:
                            dense(6, 6, 390, 3, [16, 17], 384)
                            dense(0, 0, 0, 1, [16, 17], 0)
                        if part == 0:
                            ty = 0 if i == 0 else (2 if i == 15 else 1)
                            prv = i - 1 if i > 0 else None
                            nxt = i + 1 if i < 15 else None
                        else:
                            ty = 0 if i == 16 else 2
                            prv = 16 if i == 17 else None
                            nxt = 17 if i == 16 else None
                        pp = rA2.next()
                        for g in range(4):
                            terms = [(i, 5 * g + ty)]
                            if prv is not None:
                                terms.append((prv, 5 * g + 3))
                            if nxt is not None:
                                terms.append((nxt, 5 * g + 4))
                            for ti, (tile_, mi) in enumerate(terms):
                                P.mm(pp[0:64, 128 * g:128 * g + 128], pcS[:, 256 * tile_ + 64 * g:256 * tile_ + 64 * g + 64], MT[:, 128 * mi:128 * mi + 128],
                                     start=(ti == 0), stop=(ti == len(terms) - 1))
                        P.cp(plT, pp[0:64, :], eng="act")
                        pq = rA2.next()
                        for g in range(4):
                            P.mm(pq[:, 64 * g:64 * g + 64], plT[:, 128 * g:128 * g + 128], wpl[:, 64 * g:64 * g + 64], start=True, stop=True)
                        P.tt(Yt[:, 768:1024], pq[:, 0:256], psc, ALU.mult)
                        for c in (range(3, 8) if part == 0 else range(8)):
                            P.tr(ptY[:, 128 * c:128 * c + 128], Yt[:, 128 * c:128 * c + 128], ident)
                        P.cp(YTs, ptY, eng="act")
                        for ng in range(2):
                            acc = None
                            for br, (c0, c1) in enumerate(((0, 3), (3, 6), (6, 8))):
                                p = rA2.next()
                                for c in range(c0, c1):
                                    P.mm(p, YTs[:, 128 * c:128 * c + 128], Wbr[:, c * D + 512 * ng:c * D + 512 * ng + 512], start=(c == c0), stop=(c == c1 - 1))
                                gsl = gtB[:, D * br + 512 * ng:D * br + 512 * ng + 512]
                                if br == 0:
                                    acc = e1.next()
                                    P.tt(acc, p, gsl, ALU.mult)
                                else:
                                    tmp = e1.next()
                                    P.tt(tmp, p, gsl, ALU.mult)
                                    if br == 1:
                                        acc2 = e1.next()
                                        P.tt(acc2, acc, tmp, ALU.add)
                                        acc = acc2
                                    else:
                                        P.tt(mb[:, 512 * ng:512 * ng + 512], acc, tmp, ALU.add)
                        pt = psT[1]
                        for k in range(8):
                            P.tr(pt[:, 128 * k:128 * k + 128], mb[:, 128 * k:128 * k + 128], ident)
                        P.cp(mTs, pt, eng="act")
                        for ng in range(2):
                            p = rA2.next()
                            for k in range(8):
                                P.mm(p, mTs[:, 128 * k:128 * k + 128], Wout[:, k * D + 512 * ng:k * D + 512 * ng + 512], start=(k == 0), stop=(k == 7))
                            tmp = e1.next()
                            P.tt(tmp, p, GXt[:, 512 * ng:512 * ng + 512], ALU.mult)
                            P.tt(xt_[:, 512 * ng:512 * ng + 512], xt_[:, 512 * ng:512 * ng + 512], tmp, ALU.add)
                        P.dma(xres[b][128 * i:128 * i + 128, :], xt_, concurrent=True)

            cur_fence[0] = fence()
            W1 = carve(AR, "W1", 0, 128, 0, 8 * DFF)
            W3 = carve(AR, "W3", 0, 128, 8 * DFF, 8 * DFF)
            W2 = carve(AR, "W2", 0, 128, 16 * DFF, 22 * D)
            gg = carve(WKB, "gg", 0, 128, 0, DFF)
            gT = carve(WKB, "gT", 0, 128, DFF, 22 * 128)
            rUT = Rot([uT, carve(WKB, "uT2", 0, 128, 2 * DFF, D)])
            GXt = carve(WKF, "GXt", 0, 128, 0, D)
            load_mat(W1, w_ff1[l], 8, DFF)
            load_mat(W3, w_ff3[l], 8, DFF)
            load_mat(W2, w_ff2[l], 22, D)
            def c_S1(uT_):
                for g in range(6):
                    c0 = 512 * g
                    n = min(512, DFF - c0)
                    p1 = rA4.next()
                    p3 = rA4.next()
                    for k in range(8):
                        P.mm(p1[:, 0:n], uT_[:, 128 * k:128 * k + 128], W1[:, k * DFF + c0:k * DFF + c0 + n], start=(k == 0), stop=(k == 7))
                    for k in range(8):
                        P.mm(p3[:, 0:n], uT_[:, 128 * k:128 * k + 128], W3[:, k * DFF + c0:k * DFF + c0 + n], start=(k == 0), stop=(k == 7))
                    e = e1.next()
                    P.act(e[:, 0:n], p1[:, 0:n], AF.Silu)
                    P.tt(gg[:, c0:c0 + n], p3[:, 0:n], e[:, 0:n], ALU.mult)

            def c_S2():
                for j in range(3):
                    pt = rT.next()
                    nk = 8 if j < 2 else 6
                    for kk in range(nk):
                        k = 8 * j + kk
                        P.tr(pt[:, 128 * kk:128 * kk + 128], gg[:, 128 * k:128 * k + 128], ident)
                    P.cp(gT[:, 1024 * j:1024 * j + 128 * nk], pt[:, 0:128 * nk], eng="act")

            def c_S3(b, i, xt_):
                for ng in range(2):
                    po = rA4.next()
                    for k in range(22):
                        P.mm(po, gT[:, 128 * k:128 * k + 128], W2[:, k * D + 512 * ng:k * D + 512 * ng + 512], start=(k == 0), stop=(k == 21))
                    e = e1.next()
                    P.tt(e, po, GXt[:, 512 * ng:512 * ng + 512], ALU.mult)
                    P.tt(xt_[:, 512 * ng:512 * ng + 512], xt_[:, 512 * ng:512 * ng + 512], e, ALU.add)
                if last:
                    finals.append(P.dma(out[b, 128 * i:128 * i + 128, :], xt_, concurrent=True))
                else:
                    P.dma(xres[b][128 * i:128 * i + 128, :], xt_, concurrent=True)

            def c_S0a(b, i):
                xt_ = rX.next()
                P.dma(xt_, xres[b][128 * i:128 * i + 128, :])
                norm_a(xt_)
                return xt_

            for b in range(bl):
                for part in range(2):
                    if part == 1 and last:
                        continue
                    slot = b if part == 0 else 4
                    load_mod(l, slot, 3 * D, 2 * D, modt)
                    load_mod(l, slot, 5 * D, D, GXt)
                    tl = list(range(16) if part == 0 else range(16, 18))
                    xt_ = c_S0a(b, tl[0])
                    uT_ = norm_b(rUT)
                    for ix, i in enumerate(tl):
                        nxt_ = tl[ix + 1] if ix + 1 < len(tl) else None
                        if nxt_ is not None:
                            xtn = c_S0a(b, nxt_)
                        c_S1(uT_)
                        if nxt_ is not None:
                            uT_ = norm_b(rUT)
                        c_S2()
                        c_S3(b, i, xt_)
                        if nxt_ is not None:
                            xt_ = xtn
        P.finalize(st, final_waits=finals)
    return nc


def _band_matrices():
    N = 384
    out = np.zeros((128, 20, 128), np.float32)
    t = np.arange(N)
    for g, win in enumerate(POOL_WINDOWS):
        lo = np.clip(t - win // 2, 0, N)
        hi = np.clip(t + win // 2, 0, N)
        M = np.zeros((N, N), np.float32)
        for tt_ in range(N):
            M[tt_, lo[tt_]:hi[tt_]] = 1.0 / float(hi[tt_] - lo[tt_])
            M[tt_, tt_] -= 1.0
        out[:, 5 * g + 0, :] = M[0:128, 0:128].T
        out[:, 5 * g + 1, :] = M[128:256, 128:256].T
        out[:, 5 * g + 2, :] = M[256:384, 256:384].T
        out[:, 5 * g + 3, :] = M[128:256, 0:128].T
        out[:, 5 * g + 4, :] = M[0:128, 128:256].T
    return out.reshape(128, 20 * 128)


def _rope_tables():
    t = np.arange(S)
    row = (t // GRID_W).astype(np.float32)
    col = (t % GRID_W).astype(np.float32)
    freqs = (10000.0 ** (-np.arange(16, dtype=np.float32) / 16)).astype(np.float32)
    ang = np.concatenate([row[:, None] * freqs[None, :], col[:, None] * freqs[None, :]], axis=1)
    c = np.cos(ang).astype(np.float32).reshape(16, 128, 32).transpose(1, 0, 2).reshape(128, 512)
    s = np.sin(ang).astype(np.float32).reshape(16, 128, 32).transpose(1, 0, 2).reshape(128, 512)
    return np.ascontiguousarray(c), np.ascontiguousarray(s)


_PROGS = {}
IN_PERM = np.concatenate([np.arange(0, 384), np.arange(1152, 1536), np.arange(384, 768), np.arange(1536, 1664),
                          np.arange(1792, 2048), np.arange(768, 1152), np.arange(1664, 1792), np.arange(2048, 5120)])


def prep_shared(w_mod, b_mod, norm1_g, norm2_g, w_in, q_norm_a, k_norm_a, q_norm_b, k_norm_b, rpb_a, w_pool,
                pool_scale, w_br_a, w_br_b, w_br_c, w_out, w_ff1, w_ff3, w_ff2, depth=DEPTH):
    f = lambda a: np.ascontiguousarray(np.asarray(a, dtype=np.float32)[:depth])
    rc, rs = _rope_tables()
    cq = np.arange(GRID_W)
    col_start = np.clip(cq - 8, 0, GRID_W - 16)
    col_valid = (cq[None, :] >= col_start[:, None]) & (cq[None, :] < col_start[:, None] + 16)
    dc_idx = np.clip(cq[None, :] - cq[:, None] + 15, 0, 30)
    rpb = f(rpb_a)
    G = np.empty((depth, 2, 64, 6, 14, 64), np.float32)
    for j in range(2):
        for dr0 in range(14):
            G[:, j, :, :, dr0, :] = rpb[:, :, dr0 + j, :][:, :, dc_idx].transpose(0, 3, 1, 2)
    cm = np.ascontiguousarray(np.tile(col_valid.T.astype(np.float32), (2, 1)))
    qn_a, kn_a, qn_b, kn_b = f(q_norm_a), f(k_norm_a), f(q_norm_b), f(k_norm_b)
    qkg = np.concatenate([np.tile(qn_a, (1, 6)), np.tile(qn_b, (1, 6)), np.tile(kn_a, (1, 6)), np.tile(kn_b, (1, 2))], axis=1)
    return {
        "w_mod": f(w_mod), "b_mod": f(b_mod),
        "ng12": np.ascontiguousarray(np.concatenate([f(norm1_g), f(norm2_g)], axis=1)),
        "w_in": np.ascontiguousarray(f(w_in)[:, :, IN_PERM]),
        "qkg": np.ascontiguousarray(qkg), "ropec": rc, "ropes": rs,
        "erpb": np.ascontiguousarray(G.reshape(depth, 128, 6 * 14 * 64)), "cmask": cm,
        "bandm": _band_matrices(),
        "wpool": np.ascontiguousarray(f(w_pool).transpose(0, 2, 1, 3).reshape(depth, 64, 256)),
        "pscale": f(pool_scale),
        "w_br": np.ascontiguousarray(np.concatenate([f(w_br_a), f(w_br_b), f(w_br_c)], axis=1)),
        "w_out": f(w_out), "w_ff1": f(w_ff1), "w_ff3": f(w_ff3), "w_ff2": f(w_ff2),
        "ident": np.eye(128, dtype=np.float32),
    }


def prep_core(x, c, ctx, c_ctx, b0, bl):
    xin = np.concatenate([x[b0:b0 + bl], ctx[b0:b0 + bl]], axis=1)
    cv = np.zeros((5, D), np.float32)
    cv[0:bl] = c[b0:b0 + bl]
    cv[4] = c_ctx
    cT = np.ascontiguousarray(cv.reshape(5, 8, 128).transpose(2, 1, 0)).reshape(128, 40)
    return {"xin": np.ascontiguousarray(xin), "cT": cT}


def kernel(x, c, ctx, c_ctx, w_mod, b_mod, norm1_g, norm2_g, w_in, q_norm_a, k_norm_a,
           q_norm_b, k_norm_b, rpb_a, w_pool, pool_scale, w_br_a, w_br_b, w_br_c, w_out,
           w_ff1, w_ff3, w_ff2):
    f = lambda a: np.ascontiguousarray(np.asarray(a, dtype=np.float32))
    x, c, ctx, c_ctx = f(x), f(c), f(ctx), f(c_ctx)
    if (DEPTH, BL) not in _PROGS:
        _PROGS[(DEPTH, BL)] = build_program(DEPTH, BL)
    nc = _PROGS[(DEPTH, BL)]
    shared = prep_shared(w_mod, b_mod, norm1_g, norm2_g, w_in, q_norm_a, k_norm_a, q_norm_b, k_norm_b, rpb_a, w_pool,
                         pool_scale, w_br_a, w_br_b, w_br_c, w_out, w_ff1, w_ff3, w_ff2)
    in_maps = []
    for core in range(NCORES):
        m = prep_core(x, c, ctx, c_ctx, core * BL, BL)
        m.update(shared)
        in_maps.append(m)
    res = run_bass_kernel_spmd(nc, in_maps, core_ids=list(range(NCORES)))
    return np.concatenate([np.asarray(r["out"], dtype=np.float32) for r in res.results], axis=0)
```
